# Optimizing a Trainium2 kernel written in Bass

```python
import jax, jax.numpy as jnp
from jax import lax
import numpy as np

D_MODEL = 1024
BATCH = 4
SEQ = 8192
DEPTH = 1

D_RNN = 1024
RNN_HEADS = 16
RNN_HEAD_DIM = D_RNN // RNN_HEADS
CONV_WIDTH = 4
RG_C = 8.0
D_SGU = 1024
SGU_GROUPS = 8
SGU_GROUP_DIM = D_SGU // SGU_GROUPS
CHUNK = 128
IN_COLS = 2 * D_RNN + 2 * D_SGU + 2 * D_MODEL
SPLIT_POINTS = (D_RNN, 2 * D_RNN, 2 * D_RNN + D_SGU, 2 * D_RNN + 2 * D_SGU, 2 * D_RNN + 2 * D_SGU + D_MODEL)
D_FF = -(-8 * D_MODEL // (3 * 256)) * 256
EPS = 1e-6

kernel_name = "hybrid_rglru_chunked_sgu_gated_block"


def rms_norm(x, g):
    xf = x.astype(jnp.float32)
    y = xf * lax.rsqrt(jnp.mean(xf * xf, axis=-1, keepdims=True) + EPS)
    return (y * g.astype(jnp.float32)).astype(x.dtype)


def layer_norm(x, g, b):
    xf = x.astype(jnp.float32)
    mu = jnp.mean(xf, axis=-1, keepdims=True)
    var = jnp.mean(jnp.square(xf - mu), axis=-1, keepdims=True)
    y = (xf - mu) * lax.rsqrt(var + EPS)
    return (y * g.astype(jnp.float32) + b.astype(jnp.float32)).astype(x.dtype)


def causal_depthwise_conv(x, w, b):
    s = x.shape[1]
    xp = jnp.pad(x, ((0, 0), (CONV_WIDTH - 1, 0), (0, 0)))
    y = b
    for k in range(CONV_WIDTH):
        y = y + xp[:, k:k + s, :] * w[k]
    return y


def rg_lru(x, w_a, b_a, w_x, b_x, lam):
    bsz, s, _ = x.shape
    xh = x.reshape(bsz, s, RNN_HEADS, RNN_HEAD_DIM)
    r = jax.nn.sigmoid(jnp.einsum('bshd,hde->bshe', xh, w_a) + b_a).reshape(bsz, s, D_RNN)
    i = jax.nn.sigmoid(jnp.einsum('bshd,hde->bshe', xh, w_x) + b_x).reshape(bsz, s, D_RNN)
    log_a = -RG_C * r.astype(jnp.float32) * jax.nn.softplus(-lam.astype(jnp.float32))
    a = jnp.exp(log_a)
    gated_x = jnp.sqrt(-jnp.expm1(2.0 * log_a)) * (i * x).astype(jnp.float32)

    def combine(c1, c2):
        a1, b1 = c1
        a2, b2 = c2
        return a1 * a2, a2 * b1 + b2

    _, h = lax.associative_scan(combine, (a, gated_x), axis=1)
    return h.astype(x.dtype)


def chunked_sgu(u, v, ln_g, ln_b, w_s, b_s):
    bsz, s, _ = v.shape
    n_chunks = s // CHUNK
    v = layer_norm(v, ln_g, ln_b)
    vc = v.reshape(bsz, n_chunks, CHUNK, SGU_GROUPS, SGU_GROUP_DIM)
    causal = jnp.tril(jnp.ones((CHUNK, CHUNK), dtype=bool))
    w = jnp.where(causal[None], w_s, jnp.zeros_like(w_s))
    mixed = jnp.einsum('gts,bnsgd->bntgd', w, vc) + b_s.T[None, None, :, :, None]
    return u * mixed.reshape(bsz, s, D_SGU)


def setup_inputs(seed: int = 0) -> dict:
    key = jax.random.key(seed)
    ks = jax.random.split(key, 24)
    f32 = jnp.float32

    def nrm(k, shape, scale):
        return jax.random.normal(k, shape, f32) * scale

    a_c = jax.random.uniform(ks[10], (DEPTH, D_RNN), f32, 0.9, 0.999)
    p = a_c ** (1.0 / RG_C)
    rg_lambda = jnp.log(p) - jnp.log1p(-p)
    return {
        "x": nrm(ks[0], (BATCH, SEQ, D_MODEL), 1.0),
        "norm_mix_g": 1.0 + nrm(ks[1], (DEPTH, D_MODEL), 0.02),
        "w_in": nrm(ks[2], (DEPTH, D_MODEL, IN_COLS), D_MODEL ** -0.5),
        "conv_w": nrm(ks[3], (DEPTH, CONV_WIDTH, D_RNN), CONV_WIDTH ** -0.5),
        "conv_b": nrm(ks[4], (DEPTH, D_RNN), 0.01),
        "rg_wa": nrm(ks[5], (DEPTH, RNN_HEADS, RNN_HEAD_DIM, RNN_HEAD_DIM), RNN_HEAD_DIM ** -0.5),
        "rg_ba": nrm(ks[6], (DEPTH, RNN_HEADS, RNN_HEAD_DIM), 0.01),
        "rg_wx": nrm(ks[7], (DEPTH, RNN_HEADS, RNN_HEAD_DIM, RNN_HEAD_DIM), RNN_HEAD_DIM ** -0.5),
        "rg_bx": nrm(ks[8], (DEPTH, RNN_HEADS, RNN_HEAD_DIM), 0.01),
        "rg_lambda": rg_lambda,
        "sgu_ln_g": 1.0 + nrm(ks[11], (DEPTH, D_SGU), 0.02),
        "sgu_ln_b": nrm(ks[12], (DEPTH, D_SGU), 0.01),
        "sgu_ws": nrm(ks[13], (DEPTH, SGU_GROUPS, CHUNK, CHUNK), CHUNK ** -0.5),
        "sgu_bs": 1.0 + nrm(ks[14], (DEPTH, SGU_GROUPS, CHUNK), 0.02),
        "w_proj_a": nrm(ks[15], (DEPTH, D_RNN, D_MODEL), D_RNN ** -0.5),
        "w_proj_b": nrm(ks[16], (DEPTH, D_SGU, D_MODEL), D_SGU ** -0.5),
        "w_out": nrm(ks[17], (DEPTH, D_MODEL, D_MODEL), D_MODEL ** -0.5),
        "norm_ffn_g": 1.0 + nrm(ks[18], (DEPTH, D_MODEL), 0.02),
        "w_gate_up": nrm(ks[19], (DEPTH, D_MODEL, 2 * D_FF), D_MODEL ** -0.5),
        "w_down": nrm(ks[20], (DEPTH, D_FF, D_MODEL), D_FF ** -0.5),
        "norm_final_g": 1.0 + nrm(ks[21], (D_MODEL,), 0.02),
    }


def reference(x, norm_mix_g, w_in, conv_w, conv_b, rg_wa, rg_ba, rg_wx, rg_bx, rg_lambda,
              sgu_ln_g, sgu_ln_b, sgu_ws, sgu_bs, w_proj_a, w_proj_b, w_out,
              norm_ffn_g, w_gate_up, w_down, norm_final_g):
    for l in range(DEPTH):
        h = rms_norm(x, norm_mix_g[l])
        proj = h @ w_in[l]
        rnn_x, rnn_gate, sgu_u, sgu_v, gate_a, gate_b = jnp.split(proj, SPLIT_POINTS, axis=-1)
        rnn_x = causal_depthwise_conv(rnn_x, conv_w[l], conv_b[l])
        y_a = jax.nn.gelu(rnn_gate) * rg_lru(rnn_x, rg_wa[l], rg_ba[l], rg_wx[l], rg_bx[l], rg_lambda[l])
        y_b = chunked_sgu(jax.nn.gelu(sgu_u), jax.nn.gelu(sgu_v), sgu_ln_g[l], sgu_ln_b[l], sgu_ws[l], sgu_bs[l])
        merged = jax.nn.sigmoid(gate_a) * (y_a @ w_proj_a[l]) + jax.nn.sigmoid(gate_b) * (y_b @ w_proj_b[l])
        x = x + merged @ w_out[l]
        h = rms_norm(x, norm_ffn_g[l])
        g, u = jnp.split(h @ w_gate_up[l], 2, axis=-1)
        x = x + (jax.nn.silu(g) * u) @ w_down[l]
    return rms_norm(x, norm_final_g)
```

```python
import numpy as np
import concourse.bass as bass
import concourse.mybir as mybir
from concourse.bass_utils import run_bass_kernel_spmd

F32 = mybir.dt.float32
BF16 = mybir.dt.bfloat16
AF = mybir.ActivationFunctionType
ALU = mybir.AluOpType

P = 128
D = 1024
KC = 8
T = 512
NJ = 4
DFF = 2816
NF = 22
EPS = 1e-6
SAME_ENG_SYNC = True
RELAX_SAME_ENG = False
USE_POOL = True
HOIST_V = True
PIPE_NEXT = False
IO_ON_POOL = True
NRING = 4

ENGS = ["pe", "act", "dve", "pool", "sp"]


class Buf:
    __slots__ = ("name", "w", "r", "small")

    def __init__(self, name, small=False):
        self.name = name
        self.w = None
        self.r = {}
        self.small = small


class Sched:
    def __init__(self):
        self.streams = {e: [] for e in ENGS}
        self.count = {e: 0 for e in ENGS}
        self.seen = {e: {} for e in ENGS}
        self.dmacount = {}
        self.skip = False

    def _deps(self, eng, reads, writes):
        deps = {}

        def add(d, small):
            if d is None:
                return
            k, v = d
            if k == eng and not small and RELAX_SAME_ENG:
                return
            if deps.get(k, 0) < v:
                deps[k] = v

        for b in reads:
            add(b.w, b.small)
        for b in writes:
            add(b.w, b.small)
            for k, v in b.r.items():
                add((k, v), b.small)
        waits = []
        for k, v in deps.items():
            if k == eng and (eng == "pe" or not SAME_ENG_SYNC):
                continue
            if self.seen[eng].get(k, 0) >= v:
                continue
            self.seen[eng][k] = v
            waits.append((k, v))
        return waits

    def op(self, eng, emit, reads=(), writes=()):
        if self.skip:
            return
        waits = self._deps(eng, reads, writes)
        self.count[eng] += 1
        v = self.count[eng]
        self.streams[eng].append((waits, emit, (eng, 1)))
        for b in reads:
            if b.r.get(eng, 0) < v:
                b.r[eng] = v
        for b in writes:
            b.w = (eng, v)
            b.r = {}

    def dma(self, eng, emit, sem, reads=(), writes=()):
        if self.skip:
            self.dmacount.setdefault(sem, 0)
            return
        waits = self._deps(eng, reads, writes)
        self.dmacount[sem] = self.dmacount.get(sem, 0) + 16
        v = self.dmacount[sem]
        self.streams[eng].append((waits, emit, (sem, 16)))
        for b in reads:
            if b.r.get(sem, 0) < v:
                b.r[sem] = v
        for b in writes:
            b.w = (sem, v)
            b.r = {}


def build_nc(n_main, n_pre):
    import os
    DBG = os.environ.get('KDBG', '')
    nc = bass.Bass("TRN2", target_bir_lowering=False)
    NTOK = n_main * T
    NPTOK = max(n_pre, 1) * T

    def din(name, shape):
        return nc.dram_tensor(name, list(shape), F32, kind="ExternalInput")

    xm_t = din("xm", [NTOK, D])
    xp_t = din("xp", [NPTOK, D])
    flag_t = din("flag", [P, 1])
    g1_t = din("norm_mix_g", [1, D])
    win_t = din("w_in", [1, D, 6 * D])
    cw_t = din("conv_w", [1, 4, D])
    cb_t = din("conv_b", [1, D])
    wa_t = din("rg_wa", [1, 16, 64, 64])
    ba_t = din("rg_ba", [1, 16, 64])
    wx_t = din("rg_wx", [1, 16, 64, 64])
    bx_t = din("rg_bx", [1, 16, 64])
    lam_t = din("rg_lambda", [1, D])
    lg_t = din("sgu_ln_g", [1, D])
    lb_t = din("sgu_ln_b", [1, D])
    ws_t = din("sgu_ws", [1, 8, P, P])
    bs_t = din("sgu_bs", [1, 8, P])
    wpa_t = din("w_proj_a", [1, D, D])
    wpb_t = din("w_proj_b", [1, D, D])
    wo_t = din("w_out", [1, D, D])
    g2_t = din("norm_ffn_g", [1, D])
    wgu_t = din("w_gate_up", [1, D, 2 * DFF])
    wd_t = din("w_down", [1, DFF, D])
    gf_t = din("norm_final_g", [D])
    out_t = nc.dram_tensor("out", [NTOK, D], F32, kind="ExternalOutput")

    S_A = 0
    S_V = 8
    S_U = 10
    S_M = 12
    S_O = 20
    S_F = 22
    S_D = 33
    NS = 39
    scr_t = nc.dram_tensor("wscr", [NS, P, KC, 512], BF16, kind="Internal")

    sch = Sched()
    sems = {}

    A = {}

    def sb(name, shape, dt):
        A[name] = nc.alloc_sbuf_tensor(name, list(shape), dt)
        return A[name]

    X = [sb(f"X{b}", [P, NJ, D], F32) for b in range(2)]
    TMB = sb("TMB", [P, NJ, D], BF16)
    XT = [sb(f"XT{b}", [P, KC, T], BF16) for b in range(2)]
    RXT = [sb(f"RXT{q}", [P, 516], F32) for q in range(4)]
    HALO = sb("HALO", [P, KC, 4], F32)
    XCB = sb("XCB", [P, KC, T], BF16)
    ACC = [sb(f"ACC{q}", [P, T], F32) for q in range(2)]
    RA = [sb(f"RA{q}", [P, T], F32) for q in range(2)]
    A2 = [sb(f"A2{q}", [P, T], F32) for q in range(2)]
    IG = [sb(f"IG{q}", [P, T], F32) for q in range(2)]
    HH = [sb(f"HH{q}", [P, T], F32) for q in range(2)]
    GG = [sb(f"GG{q}", [P, T], BF16) for q in range(2)]
    HS = sb("HS", [P, KC], F32)
    YA = sb("YA", [P, KC, T], BF16)
    YB = sb("YB", [P, KC, T], BF16)
    MG = sb("MG", [P, KC, T], BF16)
    HM = sb("HM", [P, NF * T], BF16)
    HM32 = HM.bitcast(F32)
    GU = [sb(f"GU{q}", [P, T], BF16) for q in range(2)]
    T1 = [sb(f"T1{q}", [P, T], F32) for q in range(2)]
    SA = [sb(f"SA{q}", [P, T], F32) for q in range(2)]
    SB_ = [sb(f"SB{q}", [P, T], F32) for q in range(2)]
    SG = [sb(f"SG{q}", [P, T], BF16) for q in range(2)]
    JUNK = sb("JUNK", [P, D], BF16)
    RING = [sb(f"RING{r}", [P, KC, 512], BF16) for r in range(NRING)]
    IDF = sb("IDF", [P, P], F32)
    IDB = sb("IDB", [P, P], BF16)
    VS = sb("VS", [P, P], F32)
    CV = sb("CV", [P, 96], F32)
    CX = sb("CX", [P, 64], F32)
    WAB = sb("WAB", [P, KC, P], BF16)
    WXB = sb("WXB", [P, KC, P], BF16)
    WST = sb("WST", [P, 8, P], BF16)
    CBI = sb("CBI", [P, 8, P], F32)
    GFB = sb("GFB", [P, D], F32)
    FLG = sb("FLG", [P, 1], F32)
    ST = [sb(f"ST{q}", [P, 64], F32) for q in range(2)]

    PS = [nc.alloc_psum_tensor(f"ps{i}", [P, 512], F32) for i in range(8)]
    PSB = [p.bitcast(BF16) for p in PS]

    bX = [[Buf(f"X{b}_{j}") for j in range(NJ)] for b in range(2)]
    bTMB = [Buf(f"TMB{j}") for j in range(NJ)]
    bXT = [[Buf(f"XT{b}_{k}") for k in range(KC)] for b in range(2)]
    bRXT = [Buf(f"RXT{q}") for q in range(4)]
    bHALO = [Buf(f"HALO{c}", True) for c in range(KC)]
    bXCB = [Buf(f"XCB{c}") for c in range(KC)]
    bACC = [Buf(f"ACC{q}") for q in range(2)]
    bRA = [Buf(f"RA{q}") for q in range(2)]
    bA2 = [Buf(f"A2{q}") for q in range(2)]
    bIG = [Buf(f"IG{q}") for q in range(2)]
    bHH = [Buf(f"HH{q}") for q in range(2)]
    bGG = [Buf(f"GG{q}") for q in range(2)]
    bHS = [Buf(f"HS{c}", True) for c in range(KC)]
    bYA = [Buf(f"YA{c}") for c in range(KC)]
    bYB = [Buf(f"YB{c}") for c in range(KC)]
    bMG = [Buf(f"MG{c}") for c in range(KC)]
    bHM = [Buf(f"HM{f}") for f in range(NF)]
    bGU = [Buf(f"GU{q}") for q in range(2)]
    bT1 = [Buf(f"T1{q}") for q in range(2)]
    bSA = [Buf(f"SA{q}") for q in range(2)]
    bSB = [Buf(f"SB{q}") for q in range(2)]
    bSG = [Buf(f"SG{q}") for q in range(2)]
    bJUNK = Buf("JUNK")
    bRING = [Buf(f"RING{r}") for r in range(NRING)]
    bPS = [Buf(f"PS{i}") for i in range(8)]
    bCONST = Buf("CONST", True)
    bSCRA = Buf("SCRA")
    bSCRB = Buf("SCRB")
    bSETUP = Buf("SETUP", True)
    bST = {}

    def stbuf(par, name):
        key = (par, name)
        if key not in bST:
            bST[key] = Buf(f"ST{par}_{name}", True)
        return bST[key]

    def GVap(j, lo, hi):
        return HM32[:, j * D + lo: j * D + hi]

    def bGV(j):
        return [bHM[4 * j + i] for i in range(4)]

    psrot = [0]

    def ps_next():
        i = psrot[0]
        psrot[0] = (i + 1) % 8
        return i

    ring_state = {"n": 0, "pending_store": None, "since": 0}

    def wload(slot, width, scr_buf, nk=KC, col0=0):
        r = ring_state["n"] % NRING
        ring_state["n"] += 1
        src = scr_t.ap()[slot, :, 0:nk, col0:col0 + width]
        dst = RING[r][:, 0:nk, 0:width]
        sch.dma("sp", lambda e, dst=dst, src=src: e.dma_start(out=dst, in_=src), f"w{r}",
                reads=list(scr_buf), writes=[bRING[r]])
        ring_state["since"] += 1
        if ring_state["pending_store"] is not None and ring_state["since"] >= NRING:
            ring_state["pending_store"]()
            ring_state["pending_store"] = None
        if ring_state["pending_store"] is None and ring_state.get("pending_load") is not None:
            ring_state["pending_load"]()
            ring_state["pending_load"] = None
        return r

    def cst(v, k):
        return CV[:, v * 8 + k: v * 8 + k + 1]

    V_G1, V_G2, V_CW0, V_CB, V_BA, V_BX, V_LAM, V_LG = 0, 1, 2, 6, 7, 8, 9, 10
    X_C1, X_HC1, X_HBA, X_HBX = 0, 8, 16, 24

    def cx(base, k):
        return CX[:, base + k: base + k + 1]

    KSTAGE = int(os.environ.get('KSTAGE', '99'))

    def stage(n):
        sch.skip = n > KSTAGE

    stage(1)
    bZ = Buf("Z", True)

    def cdma(dst, src):
        sch.dma("sp", lambda e: e.dma_start(out=dst, in_=src), "cst", reads=[bZ], writes=[])
        bCONST.w = ("cst", sch.dmacount["cst"])

    sch.op("dve", lambda e: e.memset(VS[:], 0.0), writes=[bZ])
    sch.op("dve", lambda e: e.memset(HM32[:, 2048:4096], 0.0), writes=[bZ])
    vecs = [g1_t.ap()[0], g2_t.ap()[0], cw_t.ap()[0, 0], cw_t.ap()[0, 1], cw_t.ap()[0, 2], cw_t.ap()[0, 3],
            cb_t.ap()[0], ba_t.ap()[0].rearrange("h d -> (h d)"), bx_t.ap()[0].rearrange("h d -> (h d)"),
            lam_t.ap()[0], lg_t.ap()[0]]
    for v, ap in enumerate(vecs):
        cdma(VS[v * 8:(v + 1) * 8, :], ap.rearrange("(k p) -> k p", p=P))
    cdma(FLG[:], flag_t.ap())
    cdma(GFB[:], bass.AP(gf_t, 0, [[0, P], [1, D]]))
    WS_F = HM32[:, 0:1024].rearrange("p (g s) -> p g s", g=8)
    WST_F = HM32[:, 1024:2048].rearrange("p (g s) -> p g s", g=8)
    WA_F = HM32[:, 2048:3072].rearrange("p (g s) -> p g s", g=8)
    WX_F = HM32[:, 3072:4096].rearrange("p (g s) -> p g s", g=8)
    LB_F = HM32[:, 4096:5120]
    bSTG = Buf("STG", True)
    sdma = cdma

    sdma(WS_F, ws_t.ap()[0].rearrange("g t s -> t g s"))
    for q in range(2):
        sdma(WA_F[64 * q:64 * q + 64, :, 64 * q:64 * q + 64],
             wa_t.ap()[0].rearrange("(c q) d e -> q d c e", q=2)[q])
        sdma(WX_F[64 * q:64 * q + 64, :, 64 * q:64 * q + 64],
             wx_t.ap()[0].rearrange("(c q) d e -> q d c e", q=2)[q])
    sdma(LB_F, bass.AP(lb_t, 0, [[0, P], [1, D]]))
    BSB = X[1][:, 0, :]
    sch.dma("sp", lambda e: e.dma_start(out=BSB, in_=bass.AP(bs_t, 0, [[0, P], [1, D]])), "cst",
            reads=[], writes=[bX[1][0]])
    bCONST.w = ("cst", sch.dmacount["cst"])

    stage(2)
    sch.op("pool", lambda e: e.memset(IDF[:], 0.0), writes=[bSETUP])
    sch.op("pool", lambda e: e.affine_select(out=IDF[:], in_=IDF[:], pattern=[[-1, P]], base=0,
                                             channel_multiplier=1, compare_op=ALU.not_equal, fill=1.0),
           reads=[bSETUP], writes=[bSETUP])
    sch.op("act", lambda e: e.activation(out=IDB[:], in_=IDF[:], func=AF.Copy), reads=[bSETUP], writes=[bSETUP])
    stage(3)
    sch.op("pool", lambda e: e.affine_select(out=WS_F, in_=WS_F, pattern=[[0, 8], [-1, P]], base=0,
                                             channel_multiplier=1, compare_op=ALU.is_ge, fill=0.0),
           reads=[bCONST, bSTG], writes=[bSTG])
    stage(4)
    pi = ps_next()
    sch.op("pe", lambda e, pi=pi: e.transpose(out=PS[pi][:, 0:88], in_=VS[0:88, :], identity=IDF[0:88, 0:88]),
           reads=[bCONST, bSETUP], writes=[bPS[pi]])
    sch.op("dve", lambda e, pi=pi: e.tensor_copy(out=CV[:, 0:88], in_=PS[pi][:, 0:88]),
           reads=[bPS[pi]], writes=[bSETUP])
    stage(5)
    bCXt = Buf("CXt", True)
    sch.op("act", lambda e: e.activation(out=CX[:, 32:40], in_=CV[:, V_LAM * 8:V_LAM * 8 + 8], func=AF.Exp, scale=-1.0),
           reads=[bSETUP], writes=[bCXt])
    sch.op("act", lambda e: e.activation(out=CX[:, 40:48], in_=CX[:, 32:40], func=AF.Ln, bias=1.0),
           reads=[bCXt], writes=[bCXt])
    sch.op("dve", lambda e: e.tensor_scalar(out=CX[:, X_C1:X_C1 + 8], in0=CX[:, 40:48], scalar1=-8.0, scalar2=None,
                                            op0=ALU.mult), reads=[bCXt], writes=[bSETUP])
    sch.op("dve", lambda e: e.tensor_scalar(out=CX[:, X_HC1:X_HC1 + 8], in0=CX[:, 40:48], scalar1=-4.0, scalar2=None,
                                            op0=ALU.mult), reads=[bCXt], writes=[bSETUP])
    sch.op("dve", lambda e: e.tensor_scalar(out=CX[:, X_HBA:X_HBA + 8], in0=CV[:, V_BA * 8:V_BA * 8 + 8], scalar1=0.5,
                                            scalar2=None, op0=ALU.mult), reads=[bSETUP], writes=[bSETUP])
    sch.op("dve", lambda e: e.tensor_scalar(out=CX[:, X_HBX:X_HBX + 8], in0=CV[:, V_BX * 8:V_BX * 8 + 8], scalar1=0.5,
                                            scalar2=None, op0=ALU.mult), reads=[bSETUP], writes=[bSETUP])
    stage(6)
    sch.op("act", lambda e: e.activation(out=WAB[:], in_=WA_F, func=AF.Copy), reads=[bSTG, bCONST], writes=[bSETUP])
    sch.op("act", lambda e: e.activation(out=WXB[:], in_=WX_F, func=AF.Copy), reads=[bSTG, bCONST], writes=[bSETUP])
    stage(7)
    for g in range(8):
        pi = ps_next()
        sch.op("pe", lambda e, pi=pi, g=g: e.transpose(out=PS[pi][:, 0:P], in_=WS_F[:, g, :], identity=IDF[:]),
               reads=[bSTG, bSETUP], writes=[bPS[pi]])
        if 's7a' in DBG:
            continue
        sch.op("dve", lambda e, pi=pi, g=g: e.tensor_copy(out=WST_F[:, g, :], in_=PS[pi][:, 0:P]),
               reads=[bPS[pi]], writes=[bSTG])
    if 's7b' not in DBG:
        sch.op("act", lambda e: e.activation(out=WST[:], in_=WST_F, func=AF.Copy), reads=[bSTG], writes=[bSETUP])
    stage(8)
    for g in range(8):
        pi = ps_next()

        def mm(e, pi=pi, g=g):
            return e.matmul(PS[pi][:, 0:P], lhsT=LB_F[:, g * P:(g + 1) * P], rhs=WST_F[:, g, :], start=True, stop=True)

        sch.op("pe", mm, reads=[bSTG, bSETUP, bCONST], writes=[bPS[pi]])
        sch.op("dve", lambda e, pi=pi, g=g: e.tensor_tensor(out=CBI[:, g, :], in0=PS[pi][:, 0:P],
                                                            in1=BSB[:, g * P:(g + 1) * P], op=ALU.add),
               reads=[bPS[pi], bCONST, bX[1][0]], writes=[bSETUP])
    stage(0)
    sch.op("dve", lambda e: e.memset(HS[:], 0.0), writes=bHS)
    sch.op("dve", lambda e: e.memset(HALO[:], 0.0), writes=bHALO)

    cvk = [0]
    NO_CONV = 'noconv' in DBG
    bCVs = [Buf(f"cv{i}") for i in range(4)]

    def cast_dma(slot, col0, src_ap_rows_cols):
        kc = src_ap_rows_cols.shape[0] // P
        w = src_ap_rows_cols.shape[1]
        src = src_ap_rows_cols.rearrange("(kc p) n -> p kc n", p=P)
        dst = scr_t.ap()[slot, :, 0:kc, col0:col0 + w]
        k = cvk[0]
        cvk[0] += 1
        if NO_CONV:
            return
        sch.dma("pool", lambda e: e.dma_start(out=dst, in_=src), f"cv{k % 4}", reads=[], writes=[bCVs[k % 4]])

    win = win_t.ap()[0]
    wgu = wgu_t.ap()[0]
    wdn = wd_t.ap()[0]
    for c in range(8):
        cast_dma(S_A + c, 0, win[:, c * P:(c + 1) * P])
    bSCRA_l = []
    for i in range(4):
        fb = Buf(f"scra{i}")
        fb.w = bCVs[i].w
        bSCRA_l.append(fb)
    for c in range(8):
        cast_dma(S_A + c, P, win[:, D + c * P: D + (c + 1) * P])
    for h in range(2):
        cast_dma(S_V + h, 0, win[:, 3 * D + h * 512: 3 * D + (h + 1) * 512])
    for h in range(2):
        cast_dma(S_U + h, 0, win[:, 2 * D + h * 512: 2 * D + (h + 1) * 512])
    for m in range(8):
        cast_dma(S_M + m, 0, wpa_t.ap()[0][:, m * P:(m + 1) * P])
        cast_dma(S_M + m, P, wpb_t.ap()[0][:, m * P:(m + 1) * P])
        cast_dma(S_M + m, 2 * P, win[:, 4 * D + m * P: 4 * D + (m + 1) * P])
        cast_dma(S_M + m, 3 * P, win[:, 5 * D + m * P: 5 * D + (m + 1) * P])
    for h in range(2):
        cast_dma(S_O + h, 0, wo_t.ap()[0][:, h * 512:(h + 1) * 512])
    for s in range(11):
        cast_dma(S_F + s, 0, wgu[:, s * 256:(s + 1) * 256])
        cast_dma(S_F + s, 256, wgu[:, DFF + s * 256: DFF + (s + 1) * 256])
    for h in range(2):
        for q in range(3):
            f0 = 8 * q
            f1 = min(8 * q + 8, NF)
            cast_dma(S_D + 3 * h + q, 0, wdn[f0 * P: f1 * P, h * 512:(h + 1) * 512])
    bSCRB_l = []
    for i in range(4):
        fb = Buf(f"scrb{i}")
        fb.w = bCVs[i].w
        bSCRB_l.append(fb)


    tile_ctr = [0]

    def load_x(b, src_t, t0, eng="sp"):
        src = src_t.ap()[t0:t0 + T, :].rearrange("(j p) d -> p j d", p=P)
        sem = f"x{b}" if eng == "sp" else f"xs{b}"
        sch.dma(eng, lambda e: e.dma_start(out=X[b][:], in_=src), sem, reads=[], writes=bX[b])

    def rstd_from_ssq(par, name_ssq, name_out, col_ssq, col_out, n):
        st = ST[par]
        sch.op("dve", lambda e: e.tensor_scalar(out=st[:, col_out:col_out + n], in0=st[:, col_ssq:col_ssq + n],
                                                scalar1=1.0 / D, scalar2=EPS, op0=ALU.mult, op1=ALU.add),
               reads=[stbuf(par, name_ssq)], writes=[stbuf(par, name_out)])
        sch.op("act", lambda e: e.activation(out=st[:, col_out:col_out + n], in_=st[:, col_out:col_out + n], func=AF.Sqrt),
               reads=[stbuf(par, name_out)], writes=[stbuf(par, name_out)])
        sch.op("dve", lambda e: e.reciprocal(out=st[:, col_out:col_out + n], in_=st[:, col_out:col_out + n]),
               reads=[stbuf(par, name_out)], writes=[stbuf(par, name_out)])

    def norm_part(b, par, tag, c0):
        st = ST[par]
        for j in range(NJ):
            sch.op("act", lambda e, j=j: e.activation(out=JUNK[:], in_=X[b][:, j, :], func=AF.Square,
                                                     accum_out=st[:, c0 + j:c0 + j + 1]),
                   reads=[bX[b][j]], writes=[stbuf(par, tag + "ssq")])
        rstd_from_ssq(par, tag + "ssq", tag + "rstd", c0, c0 + 4, 4)
        for j in range(NJ):
            sch.op("act", lambda e, j=j: e.activation(out=TMB[:, j, :], in_=X[b][:, j, :], func=AF.Copy,
                                                     scale=st[:, c0 + 4 + j:c0 + 5 + j]),
                   reads=[bX[b][j], stbuf(par, tag + "rstd")], writes=[bTMB[j]])

    def transpose_part(xtb, gvec):
        for k in range(KC):
            pi = ps_next()

            def tr(e, pi=pi, k=k):
                last = None
                for j in range(NJ):
                    last = e.transpose(out=PSB[pi][:, j * P:(j + 1) * P], in_=TMB[:, j, k * P:(k + 1) * P],
                                       identity=IDB[:])
                return last

            sch.op("pe", tr, reads=bTMB + [bSETUP], writes=[bPS[pi]])
            sch.op("act", lambda e, pi=pi, k=k: e.activation(out=XT[xtb][:, k, :], in_=PSB[pi][:, 0:T], func=AF.Copy,
                                                            scale=cst(gvec, k)),
                   reads=[bPS[pi], bSETUP], writes=[bXT[xtb][k]])

    def norm_and_transpose(b, par, xtb, gvec, tag, c0):
        norm_part(b, par, tag, c0)
        transpose_part(xtb, gvec)

    def fm_matmul(pi, r, col0, rhs_t, rhs_bufs):
        def mm(e):
            last = None
            for k in range(KC):
                last = e.matmul(PS[pi][:, :], lhsT=RING[r][:, k, col0:col0 + P], rhs=rhs_t[:, k, :],
                                start=(k == 0), stop=(k == KC - 1))
            return last
        sch.op("pe", mm, reads=[bRING[r], bSETUP] + rhs_bufs, writes=[bPS[pi]])

    RA4 = [RA[0], RA[1], SA[0], SA[1]]
    bRA4 = [bRA[0], bRA[1], bSA[0], bSA[1]]
    A24 = [A2[0], A2[1], SB_[0], SB_[1]]
    bA24 = [bA2[0], bA2[1], bSB[0], bSB[1]]
    IG4 = [IG[0], IG[1], T1[0], T1[1]]
    bIG4 = [bIG[0], bIG[1], bT1[0], bT1[1]]

    def branch_a(xtb, main, scr_buf, mid_hook=None):
        tt_eng = "pool" if (main and USE_POOL) else "dve"
        rr = {}

        def ph1(cs):
            for c in cs:
                q = c % 4
                qa = c % 2
                r = wload(S_A + c, P, scr_buf)
                p1 = ps_next()
                fm_matmul(p1, r, 0, XT[xtb], bXT[xtb])
                sch.op("act", lambda e, c=c, q=q: e.activation(out=RXT[q][:, 0:3], in_=HALO[:, c, 0:3], func=AF.Copy),
                       reads=[bHALO[c]], writes=[bRXT[q]])
                sch.op("act", lambda e, q=q, p1=p1: e.activation(out=RXT[q][:, 3:515], in_=PS[p1][:, :], func=AF.Copy),
                       reads=[bPS[p1]], writes=[bRXT[q]])
                sch.op("act", lambda e, c=c, q=q: e.activation(out=HALO[:, c, 0:3], in_=RXT[q][:, 512:515], func=AF.Copy),
                       reads=[bRXT[q]], writes=[bHALO[c]])
                sch.op("dve", lambda e, c=c, q=q, qa=qa: e.tensor_scalar(out=ACC[qa][:], in0=RXT[q][:, 0:512],
                                                                         scalar1=cst(V_CW0 + 0, c), scalar2=cst(V_CB, c),
                                                                         op0=ALU.mult, op1=ALU.add),
                       reads=[bRXT[q], bSETUP], writes=[bACC[qa]])
                for kk in (1, 2):
                    sch.op("dve", lambda e, c=c, q=q, qa=qa, kk=kk: e.scalar_tensor_tensor(
                        out=ACC[qa][:], in0=RXT[q][:, kk:kk + 512], scalar=cst(V_CW0 + kk, c), in1=ACC[qa][:],
                        op0=ALU.mult, op1=ALU.add), reads=[bRXT[q], bACC[qa], bSETUP], writes=[bACC[qa]])
                sch.op("dve", lambda e, c=c, q=q, qa=qa: e.scalar_tensor_tensor(
                    out=XCB[:, c, :], in0=RXT[q][:, 3:515], scalar=cst(V_CW0 + 3, c), in1=ACC[qa][:],
                    op0=ALU.mult, op1=ALU.add), reads=[bRXT[q], bACC[qa], bSETUP], writes=[bXCB[c]])

        def ph24(cs):
            for c in cs:
                q = c % 4
                p2 = ps_next()
                p3 = ps_next()
                sch.op("pe", lambda e, c=c, p2=p2: e.matmul(PS[p2][:, :], lhsT=WAB[:, c, :], rhs=XCB[:, c, :],
                                                            start=True, stop=True),
                       reads=[bXCB[c], bSETUP], writes=[bPS[p2]])
                sch.op("pe", lambda e, c=c, p3=p3: e.matmul(PS[p3][:, :], lhsT=WXB[:, c, :], rhs=XCB[:, c, :],
                                                            start=True, stop=True),
                       reads=[bXCB[c], bSETUP], writes=[bPS[p3]])
                sch.op("act", lambda e, c=c, q=q, p2=p2: e.activation(out=RA4[q][:], in_=PS[p2][:, :], func=AF.Tanh,
                                                                     scale=0.5, bias=cx(X_HBA, c)),
                       reads=[bPS[p2], bSETUP], writes=[bRA4[q]])
                sch.op("act", lambda e, c=c, q=q, p3=p3: e.activation(out=IG4[q][:], in_=PS[p3][:, :], func=AF.Tanh,
                                                                     scale=0.5, bias=cx(X_HBX, c)),
                       reads=[bPS[p3], bSETUP], writes=[bIG4[q]])
            for c in cs:
                q = c % 4
                sch.op("act", lambda e, c=c, q=q: e.activation(out=A24[q][:], in_=RA4[q][:], func=AF.Exp,
                                                              scale=cx(X_C1, c), bias=cx(X_C1, c)),
                       reads=[bRA4[q], bSETUP], writes=[bA24[q]])
                sch.op("act", lambda e, c=c, q=q: e.activation(out=RA4[q][:], in_=RA4[q][:], func=AF.Exp,
                                                              scale=cx(X_HC1, c), bias=cx(X_HC1, c)),
                       reads=[bRA4[q], bSETUP], writes=[bRA4[q]])
                sch.op("dve", lambda e, c=c, q=q: e.scalar_tensor_tensor(out=IG4[q][:], in0=IG4[q][:], scalar=1.0,
                                                                         in1=XCB[:, c, :], op0=ALU.add, op1=ALU.mult),
                       reads=[bIG4[q], bXCB[c]], writes=[bIG4[q]])
            for c in cs:
                q = c % 4
                sch.op("act", lambda e, q=q: e.activation(out=A24[q][:], in_=A24[q][:], func=AF.Sqrt, scale=-1.0, bias=1.0),
                       reads=[bA24[q]], writes=[bA24[q]])

        def ph5(cs):
            for c in cs:
                q = c % 4
                hq = c % 2
                sch.op("dve", lambda e, q=q: e.scalar_tensor_tensor(out=IG4[q][:], in0=IG4[q][:], scalar=0.5,
                                                                    in1=A24[q][:], op0=ALU.mult, op1=ALU.mult),
                       reads=[bIG4[q], bA24[q]], writes=[bIG4[q]])
                sch.op("dve", lambda e, c=c, q=q, hq=hq: e.tensor_tensor_scan(
                    out=HH[hq][:], data0=RA4[q][:], data1=IG4[q][:], initial=HS[:, c:c + 1], op0=ALU.mult, op1=ALU.add),
                    reads=[bRA4[q], bIG4[q], bHS[c]], writes=[bHH[hq]])
                sch.op("dve", lambda e, c=c, hq=hq: e.tensor_copy(out=HS[:, c:c + 1], in_=HH[hq][:, T - 1:T]),
                       reads=[bHH[hq]], writes=[bHS[c]])
                if main:
                    p4 = ps_next()
                    r4 = wload(S_A + c, P, scr_buf, col0=P)
                    fm_matmul(p4, r4, 0, XT[xtb], bXT[xtb])
                    sch.op("act", lambda e, hq=hq, p4=p4: e.activation(out=GG[hq][:], in_=PS[p4][:, :],
                                                                      func=AF.Gelu_apprx_tanh),
                           reads=[bPS[p4]], writes=[bGG[hq]])
                    sch.op(tt_eng, lambda e, c=c, hq=hq: e.tensor_tensor(out=YA[:, c, :], in0=GG[hq][:], in1=HH[hq][:],
                                                                         op=ALU.mult),
                           reads=[bGG[hq], bHH[hq]], writes=[bYA[c]])

        g0, g1 = [0, 1, 2, 3], [4, 5, 6, 7]
        ph1(g0)
        ph1(g1)
        if mid_hook is not None:
            mid_hook()
        ph24(g0)
        ph5(g0)
        ph24(g1)
        ph5(g1)

    def branch_b_v(xtb, par):
        st = ST[par]
        C_VS, C_VQ, C_MS, C_MEAN, C_M2, C_VAR, C_RSTD, C_NB = 16, 24, 28, 32, 36, 40, 44, 48
        for h in range(2):
            r = wload(S_V + h, 512, bSCRB_l)
            for j in range(NJ):
                pi = ps_next()

                def mm(e, pi=pi, r=r, j=j):
                    last = None
                    for k in range(KC):
                        last = e.matmul(PS[pi][:, :], lhsT=XT[xtb][:, k, j * P:(j + 1) * P], rhs=RING[r][:, k, :],
                                        start=(k == 0), stop=(k == KC - 1))
                    return last
                sch.op("pe", mm, reads=[bRING[r]] + bXT[xtb], writes=[bPS[pi]])
                sch.op("act", lambda e, pi=pi, j=j, h=h: e.activation(
                    out=GVap(j, h * 512, (h + 1) * 512), in_=PS[pi][:, :], func=AF.Gelu_apprx_tanh,
                    accum_out=st[:, C_VS + 2 * j + h:C_VS + 2 * j + h + 1]),
                    reads=[bPS[pi]], writes=[bGV(j)[2 * h], bGV(j)[2 * h + 1], stbuf(par, "vs")])
        for j in range(NJ):
            sch.op("act", lambda e, j=j: e.activation(out=JUNK[:], in_=GVap(j, 0, D), func=AF.Square,
                                                     accum_out=st[:, C_VQ + j:C_VQ + j + 1]),
                   reads=bGV(j), writes=[stbuf(par, "vq")])
        vs3 = st[:, C_VS:C_VS + 8].rearrange("p (j h) -> p j h", h=2)
        sch.op("dve", lambda e: e.tensor_tensor(out=st[:, C_MS:C_MS + 4], in0=vs3[:, :, 0], in1=vs3[:, :, 1], op=ALU.add),
               reads=[stbuf(par, "vs")], writes=[stbuf(par, "ms")])
        sch.op("dve", lambda e: e.tensor_scalar(out=st[:, C_MEAN:C_MEAN + 4], in0=st[:, C_MS:C_MS + 4], scalar1=1.0 / D,
                                                scalar2=None, op0=ALU.mult),
               reads=[stbuf(par, "ms")], writes=[stbuf(par, "mean")])
        sch.op("dve", lambda e: e.tensor_tensor(out=st[:, C_M2:C_M2 + 4], in0=st[:, C_MEAN:C_MEAN + 4],
                                                in1=st[:, C_MEAN:C_MEAN + 4], op=ALU.mult),
               reads=[stbuf(par, "mean")], writes=[stbuf(par, "m2")])
        sch.op("dve", lambda e: e.scalar_tensor_tensor(out=st[:, C_VAR:C_VAR + 4], in0=st[:, C_VQ:C_VQ + 4], scalar=1.0 / D,
                                                       in1=st[:, C_M2:C_M2 + 4], op0=ALU.mult, op1=ALU.subtract),
               reads=[stbuf(par, "vq"), stbuf(par, "m2")], writes=[stbuf(par, "var")])
        sch.op("dve", lambda e: e.tensor_scalar(out=st[:, C_RSTD:C_RSTD + 4], in0=st[:, C_VAR:C_VAR + 4], scalar1=EPS,
                                                scalar2=None, op0=ALU.add),
               reads=[stbuf(par, "var")], writes=[stbuf(par, "vrstd")])
        sch.op("act", lambda e: e.activation(out=st[:, C_RSTD:C_RSTD + 4], in_=st[:, C_RSTD:C_RSTD + 4], func=AF.Sqrt),
               reads=[stbuf(par, "vrstd")], writes=[stbuf(par, "vrstd")])
        sch.op("dve", lambda e: e.reciprocal(out=st[:, C_RSTD:C_RSTD + 4], in_=st[:, C_RSTD:C_RSTD + 4]),
               reads=[stbuf(par, "vrstd")], writes=[stbuf(par, "vrstd")])
        sch.op("dve", lambda e: e.scalar_tensor_tensor(out=st[:, C_NB:C_NB + 4], in0=st[:, C_MEAN:C_MEAN + 4], scalar=-1.0,
                                                       in1=st[:, C_RSTD:C_RSTD + 4], op0=ALU.mult, op1=ALU.mult),
               reads=[stbuf(par, "mean"), stbuf(par, "vrstd")], writes=[stbuf(par, "nb")])
        for j in range(NJ):
            sch.op("act", lambda e, j=j: e.activation(out=TMB[:, j, :], in_=GVap(j, 0, D), func=AF.Identity,
                                                     scale=st[:, C_RSTD + j:C_RSTD + j + 1],
                                                     bias=st[:, C_NB + j:C_NB + j + 1]),
                   reads=bGV(j) + [stbuf(par, "vrstd"), stbuf(par, "nb")], writes=[bTMB[j]])

    def branch_b_mix(xtb):
        ru = None
        for g in range(8):
            q = g % 2
            if g % 4 == 0:
                ru = wload(S_U + g // 4, 512, bSCRB_l)
            pm = ps_next()

            def mix(e, pm=pm, g=g):
                last = None
                for j in range(NJ):
                    last = e.matmul(PS[pm][:, j * P:(j + 1) * P], lhsT=TMB[:, j, g * P:(g + 1) * P], rhs=WST[:, g, :],
                                    start=True, stop=True)
                return last
            sch.op("pe", mix, reads=bTMB + [bSETUP], writes=[bPS[pm]])
            pu = ps_next()
            fm_matmul(pu, ru, (g % 4) * P, XT[xtb], bXT[xtb])
            sch.op("act", lambda e, q=q, pu=pu: e.activation(out=GU[q][:], in_=PS[pu][:, :], func=AF.Gelu_apprx_tanh),
                   reads=[bPS[pu]], writes=[bGU[q]])
            cb_ap = bass.AP(CBI, g * P, [[8 * P, P], [0, NJ], [1, P]])
            sch.op("dve", lambda e, q=q, pm=pm, g=g, cb_ap=cb_ap: e.scalar_tensor_tensor(
                out=T1[q][:].rearrange("p (j t) -> p j t", j=NJ), in0=PS[pm][:, :].rearrange("p (j t) -> p j t", j=NJ),
                scalar=cst(V_LG, g), in1=cb_ap, op0=ALU.mult, op1=ALU.add),
                reads=[bPS[pm], bSETUP], writes=[bT1[q]])
            sch.op("pool" if USE_POOL else "dve",
                   lambda e, q=q, g=g: e.tensor_tensor(out=YB[:, g, :], in0=T1[q][:], in1=GU[q][:], op=ALU.mult),
                   reads=[bT1[q], bGU[q]], writes=[bYB[g]])

    def merge(xtb):
        for m in range(8):
            q = m % 2
            r = wload(S_M + m, 512, bSCRB_l)
            pa, pb, pga, pgb = ps_next(), ps_next(), ps_next(), ps_next()
            fm_matmul(pga, r, 2 * P, XT[xtb], bXT[xtb])
            fm_matmul(pgb, r, 3 * P, XT[xtb], bXT[xtb])
            fm_matmul(pa, r, 0, YA, bYA)
            fm_matmul(pb, r, P, YB, bYB)
            sch.op("act", lambda e, q=q, pga=pga: e.activation(out=SA[q][:], in_=PS[pga][:, :], func=AF.Tanh, scale=0.5),
                   reads=[bPS[pga]], writes=[bSA[q]])
            sch.op("act", lambda e, q=q, pgb=pgb: e.activation(out=SB_[q][:], in_=PS[pgb][:, :], func=AF.Tanh, scale=0.5),
                   reads=[bPS[pgb]], writes=[bSB[q]])
            sch.op("dve", lambda e, q=q, pa=pa: e.scalar_tensor_tensor(out=SA[q][:], in0=SA[q][:], scalar=1.0,
                                                                       in1=PS[pa][:, :], op0=ALU.add, op1=ALU.mult),
                   reads=[bSA[q], bPS[pa]], writes=[bSA[q]])
            sch.op("dve", lambda e, q=q, pb=pb: e.scalar_tensor_tensor(out=SB_[q][:], in0=SB_[q][:], scalar=1.0,
                                                                       in1=PS[pb][:, :], op0=ALU.add, op1=ALU.mult),
                   reads=[bSB[q], bPS[pb]], writes=[bSB[q]])
            sch.op("pool" if USE_POOL else "dve",
                   lambda e, q=q, m=m: e.tensor_tensor(out=MG[:, m, :], in0=SA[q][:], in1=SB_[q][:], op=ALU.add),
                   reads=[bSA[q], bSB[q]], writes=[bMG[m]])

    def out_proj(b):
        for h in range(2):
            r = wload(S_O + h, 512, bSCRB_l)
            for j in range(NJ):
                pi = ps_next()

                def mm(e, pi=pi, r=r, j=j):
                    last = None
                    for k in range(KC):
                        last = e.matmul(PS[pi][:, :], lhsT=MG[:, k, j * P:(j + 1) * P], rhs=RING[r][:, k, :],
                                        start=(k == 0), stop=(k == KC - 1))
                    return last
                sch.op("pe", mm, reads=[bRING[r]] + bMG, writes=[bPS[pi]])
                sch.op("dve", lambda e, pi=pi, j=j, h=h: e.scalar_tensor_tensor(
                    out=X[b][:, j, h * 512:(h + 1) * 512], in0=PS[pi][:, :], scalar=0.5,
                    in1=X[b][:, j, h * 512:(h + 1) * 512], op0=ALU.mult, op1=ALU.add),
                    reads=[bPS[pi], bX[b][j]], writes=[bX[b][j]])

    def ffn(b, xtb, hook_a=None, hook_b=None):
        for s in range(11):
            r = wload(S_F + s, 512, bSCRB_l)
            for e2 in range(2):
                f = 2 * s + e2
                q = f % 2
                pg, pu = ps_next(), ps_next()
                fm_matmul(pg, r, e2 * P, XT[xtb], bXT[xtb])
                fm_matmul(pu, r, 256 + e2 * P, XT[xtb], bXT[xtb])
                sch.op("act", lambda e, q=q, pg=pg: e.activation(out=SG[q][:], in_=PS[pg][:, :], func=AF.Silu),
                       reads=[bPS[pg]], writes=[bSG[q]])
                sch.op("dve", lambda e, q=q, pu=pu, f=f: e.tensor_tensor(out=HM[:, f * T:(f + 1) * T], in0=PS[pu][:, :], in1=SG[q][:],
                                                                        op=ALU.mult),
                       reads=[bPS[pu], bSG[q]], writes=[bHM[f]])
        if hook_a is not None:
            hook_a()
        for h in range(2):
            pj = [ps_next() for _ in range(NJ)]
            for qq in range(3):
                nfl = min(8, NF - 8 * qq)
                r = wload(S_D + 3 * h + qq, 512, bSCRB_l, nfl)

                def mm(e, r=r, qq=qq, nfl=nfl, pj=pj):
                    last = None
                    for fl in range(nfl):
                        f = 8 * qq + fl
                        for j in range(NJ):
                            last = e.matmul(PS[pj[j]][:, :], lhsT=HM[:, f * T + j * P:f * T + (j + 1) * P], rhs=RING[r][:, fl, :],
                                            start=(f == 0), stop=(f == NF - 1))
                    return last
                sch.op("pe", mm, reads=[bRING[r]] + bHM[8 * qq:8 * qq + nfl], writes=[bPS[i] for i in pj])
            for j in range(NJ):
                sch.op("dve", lambda e, j=j, h=h, pj=pj: e.tensor_tensor(
                    out=X[b][:, j, h * 512:(h + 1) * 512], in0=PS[pj[j]][:, :], in1=X[b][:, j, h * 512:(h + 1) * 512],
                    op=ALU.add), reads=[bPS[pj[j]], bX[b][j]], writes=[bX[b][j]])

        if hook_b is not None:
            hook_b()

    def final_norm_store(b, par, t0):
        st = ST[par]
        c0 = 56
        for j in range(NJ):
            sch.op("act", lambda e, j=j: e.activation(out=JUNK[:], in_=X[b][:, j, :], func=AF.Square,
                                                     accum_out=st[:, c0 + j:c0 + j + 1]),
                   reads=[bX[b][j]], writes=[stbuf(par, "fssq")])
        rstd_from_ssq(par, "fssq", "frstd", c0, c0 + 4, 4)
        for j in range(NJ):
            sch.op("dve", lambda e, j=j: e.scalar_tensor_tensor(out=X[b][:, j, :], in0=X[b][:, j, :],
                                                                scalar=st[:, c0 + 4 + j:c0 + 5 + j], in1=GFB[:],
                                                                op0=ALU.mult, op1=ALU.mult),
                   reads=[bX[b][j], stbuf(par, "frstd"), bCONST], writes=[bX[b][j]])
        dst = out_t.ap()[t0:t0 + T, :].rearrange("(j p) d -> p j d", p=P)

        if IO_ON_POOL:
            sch.dma("pool", lambda e: e.dma_start(out=dst, in_=X[b][:]), f"o{b}", reads=bX[b], writes=[])
        else:
            def do_store():
                sch.dma("sp", lambda e: e.dma_start(out=dst, in_=X[b][:]), f"o{b}", reads=bX[b], writes=[])
            ring_state["pending_store"] = do_store
            ring_state["since"] = 0

    seq = [("pre", i) for i in range(n_pre)] + [("main", i) for i in range(n_main)]
    if 'setuponly' in DBG:
        seq = []
        load_x(0, xm_t, 0)
        dst0 = out_t.ap()[0:T, :].rearrange("(j p) d -> p j d", p=P)
        sch.op("dve", lambda e: e.tensor_copy(out=X[1][:, 0, 0:P], in_=CBI[:, 7, :]), reads=[bSETUP], writes=bX[1])
        sch.dma("sp", lambda e: e.dma_start(out=dst0, in_=X[0][:]), "o0", reads=bX[0] + bX[1], writes=[])
    if seq:
        kind, i = seq[0]
        load_x(0, xp_t if kind == "pre" else xm_t, i * T)
    if seq:
        norm_part(0, 0, "n1", 0)
        transpose_part(0, V_G1)
    for n, (kind, i) in enumerate(seq):
        b = n % 2
        par = n % 2
        has_next = n + 1 < len(seq)
        nb, npar = (n + 1) % 2, (n + 1) % 2
        if has_next:
            k2, i2 = seq[n + 1]

            def do_load(n=n, k2=k2, i2=i2, kind=kind):
                load_x((n + 1) % 2, xp_t if k2 == "pre" else xm_t, i2 * T,
                       eng="pool" if (IO_ON_POOL and kind == "main") else "sp")
            if ring_state["pending_store"] is None:
                do_load()
            else:
                ring_state["pending_load"] = do_load

        def next_norm(nb=nb, npar=npar):
            if ring_state.get("pending_load") is not None:
                if ring_state["pending_store"] is not None:
                    ring_state["pending_store"]()
                    ring_state["pending_store"] = None
                ring_state["pending_load"]()
                ring_state["pending_load"] = None
            norm_part(nb, npar, "n1", 0)

        def next_transpose():
            transpose_part(0, V_G1)

        if kind == "pre":
            branch_a(0, False, bSCRA_l)
            if i == n_pre - 1:
                sch.op("dve", lambda e: e.tensor_scalar(out=HS[:], in0=HS[:], scalar1=FLG[:, 0:1], scalar2=None,
                                                        op0=ALU.mult), reads=bHS + [bCONST], writes=bHS)
            if has_next:
                next_norm()
                next_transpose()
        else:
            if HOIST_V:
                branch_a(0, True, bSCRB_l, mid_hook=lambda par=par: branch_b_v(0, par))
            else:
                branch_a(0, True, bSCRB_l)
                branch_b_v(0, par)
            branch_b_mix(0)
            merge(0)
            out_proj(b)
            norm_and_transpose(b, par, 1, V_G2, "n2", 8)
            if PIPE_NEXT:
                ffn(b, 1, hook_a=next_norm if has_next else None, hook_b=next_transpose if has_next else None)
                final_norm_store(b, par, i * T)
            else:
                ffn(b, 1)
                final_norm_store(b, par, i * T)
                if has_next:
                    next_norm()
                    next_transpose()
    if ring_state["pending_store"] is not None:
        ring_state["pending_store"]()
        ring_state["pending_store"] = None
    final_waits = [(k, v) for k, v in sch.dmacount.items() if k.startswith("o")]

    semnames = ["pe", "act", "dve", "pool"] + sorted(sch.dmacount.keys())
    import contextlib
    with contextlib.ExitStack() as es:
        for s in semnames:
            sems[s] = es.enter_context(nc.semaphore(f"s_{s}"))
        block = es.enter_context(nc.Block())

        def run(engobj, name):
            for waits, emit, (semname, inc) in sch.streams[name]:
                for k, v in waits:
                    engobj.wait_ge(sems[k], v)
                inst = emit(engobj)
                inst.then_inc(sems[semname], inc)

        @block.tensor
        def _(e):
            run(e, "pe")

        @block.scalar
        def _(e):
            run(e, "act")

        @block.vector
        def _(e):
            run(e, "dve")

        @block.gpsimd
        def _(e):
            run(e, "pool")

        @block.sync
        def _(e):
            run(e, "sp")
            for k, v in final_waits:
                e.wait_ge(sems[k], v)
    return nc


WEIGHT_NAMES = ["norm_mix_g", "w_in", "conv_w", "conv_b", "rg_wa", "rg_ba", "rg_wx", "rg_bx", "rg_lambda",
                "sgu_ln_g", "sgu_ln_b", "sgu_ws", "sgu_bs", "w_proj_a", "w_proj_b", "w_out", "norm_ffn_g",
                "w_gate_up", "w_down", "norm_final_g"]


def run_sharded(inputs, n_main, n_pre, trace=False):
    x = np.ascontiguousarray(np.asarray(inputs["x"], dtype=np.float32))
    B, S, _ = x.shape
    half = S // 2
    assert half == n_main * T and (n_pre == 0 or half == n_pre * T)
    weights = {k: np.ascontiguousarray(np.asarray(inputs[k], dtype=np.float32)) for k in WEIGHT_NAMES}
    nc = build_nc(n_main, n_pre)
    in_maps = []
    for b in range(B):
        for hh in range(2):
            m = dict(weights)
            m["xm"] = np.ascontiguousarray(x[b, hh * half:(hh + 1) * half])
            if hh == 0:
                m["xp"] = np.zeros((max(n_pre, 1) * T, D), np.float32)
                m["flag"] = np.zeros((P, 1), np.float32)
            else:
                m["xp"] = np.ascontiguousarray(x[b, 0:half])
                m["flag"] = np.ones((P, 1), np.float32)
            in_maps.append(m)
    res = run_bass_kernel_spmd(nc, in_maps, core_ids=list(range(2 * B)), trace=trace)
    out = np.empty((B, S, D), np.float32)
    for b in range(B):
        for hh in range(2):
            out[b, hh * half:(hh + 1) * half] = res.results[2 * b + hh]["out"]
    return out, res


def kernel(**inputs):
    out, _ = run_sharded(inputs, 8, 8)
    return out
```

```python
import numpy as np
import concourse.bass as bass
import concourse.mybir as mybir
from concourse.bass_utils import run_bass_kernel_spmd

F32 = mybir.dt.float32
BF16 = mybir.dt.bfloat16
AF = mybir.ActivationFunctionType
ALU = mybir.AluOpType

P = 128
D = 1024
KC = 8
T = 512
NJ = 4
DFF = 2816
NF = 22
EPS = 1e-6
SAME_ENG_SYNC = True
RELAX_SAME_ENG = False
USE_POOL = True
HOIST_V = True
PIPE_NEXT = False
IO_ON_POOL = True
NRING = 4

ENGS = ["pe", "act", "dve", "pool", "sp"]


class Buf:
    __slots__ = ("name", "w", "r", "small")

    def __init__(self, name, small=False):
        self.name = name
        self.w = None
        self.r = {}
        self.small = small


class Sched:
    def __init__(self):
        self.streams = {e: [] for e in ENGS}
        self.count = {e: 0 for e in ENGS}
        self.seen = {e: {} for e in ENGS}
        self.dmacount = {}
        self.skip = False

    def _deps(self, eng, reads, writes):
        deps = {}

        def add(d, small):
            if d is None:
                return
            k, v = d
            if k == eng and not small and RELAX_SAME_ENG:
                return
            if deps.get(k, 0) < v:
                deps[k] = v

        for b in reads:
            add(b.w, b.small)
        for b in writes:
            add(b.w, b.small)
            for k, v in b.r.items():
                add((k, v), b.small)
        waits = []
        for k, v in deps.items():
            if k == eng and (eng == "pe" or not SAME_ENG_SYNC):
                continue
            if self.seen[eng].get(k, 0) >= v:
                continue
            self.seen[eng][k] = v
            waits.append((k, v))
        return waits

    def op(self, eng, emit, reads=(), writes=()):
        if self.skip:
            return
        waits = self._deps(eng, reads, writes)
        self.count[eng] += 1
        v = self.count[eng]
        self.streams[eng].append((waits, emit, (eng, 1)))
        for b in reads:
            if b.r.get(eng, 0) < v:
                b.r[eng] = v
        for b in writes:
            b.w = (eng, v)
            b.r = {}

    def dma(self, eng, emit, sem, reads=(), writes=()):
        if self.skip:
            self.dmacount.setdefault(sem, 0)
            return
        waits = self._deps(eng, reads, writes)
        self.dmacount[sem] = self.dmacount.get(sem, 0) + 16
        v = self.dmacount[sem]
        self.streams[eng].append((waits, emit, (sem, 16)))
        for b in reads:
            if b.r.get(sem, 0) < v:
                b.r[sem] = v
        for b in writes:
            b.w = (sem, v)
            b.r = {}


def build_nc(n_main, n_pre):
    import os
    DBG = os.environ.get('KDBG', '')
    nc = bass.Bass("TRN2", target_bir_lowering=False)
    NTOK = n_main * T
    NPTOK = max(n_pre, 1) * T

    def din(name, shape):
        return nc.dram_tensor(name, list(shape), F32, kind="ExternalInput")

    xm_t = din("xm", [NTOK, D])
    xp_t = din("xp", [NPTOK, D])
    flag_t = din("flag", [P, 1])
    g1_t = din("norm_mix_g", [1, D])
    win_t = din("w_in", [1, D, 6 * D])
    cw_t = din("conv_w", [1, 4, D])
    cb_t = din("conv_b", [1, D])
    wa_t = din("rg_wa", [1, 16, 64, 64])
    ba_t = din("rg_ba", [1, 16, 64])
    wx_t = din("rg_wx", [1, 16, 64, 64])
    bx_t = din("rg_bx", [1, 16, 64])
    lam_t = din("rg_lambda", [1, D])
    lg_t = din("sgu_ln_g", [1, D])
    lb_t = din("sgu_ln_b", [1, D])
    ws_t = din("sgu_ws", [1, 8, P, P])
    bs_t = din("sgu_bs", [1, 8, P])
    wpa_t = din("w_proj_a", [1, D, D])
    wpb_t = din("w_proj_b", [1, D, D])
    wo_t = din("w_out", [1, D, D])
    g2_t = din("norm_ffn_g", [1, D])
    wgu_t = din("w_gate_up", [1, D, 2 * DFF])
    wd_t = din("w_down", [1, DFF, D])
    gf_t = din("norm_final_g", [D])
    out_t = nc.dram_tensor("out", [NTOK, D], F32, kind="ExternalOutput")

    S_A = 0
    S_V = 8
    S_U = 10
    S_M = 12
    S_O = 20
    S_F = 22
    S_D = 33
    NS = 39
    scr_t = nc.dram_tensor("wscr", [NS, P, KC, 512], BF16, kind="Internal")

    sch = Sched()
    sems = {}

    A = {}

    def sb(name, shape, dt):
        A[name] = nc.alloc_sbuf_tensor(name, list(shape), dt)
        return A[name]

    X = [sb(f"X{b}", [P, NJ, D], F32) for b in range(2)]
    TMB = sb("TMB", [P, NJ, D], BF16)
    XT = [sb(f"XT{b}", [P, KC, T], BF16) for b in range(2)]
    RXT = [sb(f"RXT{q}", [P, 516], F32) for q in range(4)]
    HALO = sb("HALO", [P, KC, 4], F32)
    XCB = sb("XCB", [P, KC, T], BF16)
    ACC = [sb(f"ACC{q}", [P, T], F32) for q in range(2)]
    RA = [sb(f"RA{q}", [P, T], F32) for q in range(2)]
    A2 = [sb(f"A2{q}", [P, T], F32) for q in range(2)]
    IG = [sb(f"IG{q}", [P, T], F32) for q in range(2)]
    HH = [sb(f"HH{q}", [P, T], F32) for q in range(2)]
    GG = [sb(f"GG{q}", [P, T], BF16) for q in range(2)]
    HS = sb("HS", [P, KC], F32)
    YA = sb("YA", [P, KC, T], BF16)
    YB = sb("YB", [P, KC, T], BF16)
    MG = sb("MG", [P, KC, T], BF16)
    HM = sb("HM", [P, NF * T], BF16)
    HM32 = HM.bitcast(F32)
    GU = [sb(f"GU{q}", [P, T], BF16) for q in range(2)]
    T1 = [sb(f"T1{q}", [P, T], F32) for q in range(2)]
    SA = [sb(f"SA{q}", [P, T], F32) for q in range(2)]
    SB_ = [sb(f"SB{q}", [P, T], F32) for q in range(2)]
    SG = [sb(f"SG{q}", [P, T], BF16) for q in range(2)]
    JUNK = sb("JUNK", [P, D], BF16)
    RING = [sb(f"RING{r}", [P, KC, 512], BF16) for r in range(NRING)]
    IDF = sb("IDF", [P, P], F32)
    IDB = sb("IDB", [P, P], BF16)
    VS = sb("VS", [P, P], F32)
    CV = sb("CV", [P, 96], F32)
    CX = sb("CX", [P, 64], F32)
    WAB = sb("WAB", [P, KC, P], BF16)
    WXB = sb("WXB", [P, KC, P], BF16)
    WST = sb("WST", [P, 8, P], BF16)
    CBI = sb("CBI", [P, 8, P], F32)
    GFB = sb("GFB", [P, D], F32)
    FLG = sb("FLG", [P, 1], F32)
    ST = [sb(f"ST{q}", [P, 64], F32) for q in range(2)]

    PS = [nc.alloc_psum_tensor(f"ps{i}", [P, 512], F32) for i in range(8)]
    PSB = [p.bitcast(BF16) for p in PS]

    bX = [[Buf(f"X{b}_{j}") for j in range(NJ)] for b in range(2)]
    bTMB = [Buf(f"TMB{j}") for j in range(NJ)]
    bXT = [[Buf(f"XT{b}_{k}") for k in range(KC)] for b in range(2)]
    bRXT = [Buf(f"RXT{q}") for q in range(4)]
    bHALO = [Buf(f"HALO{c}", True) for c in range(KC)]
    bXCB = [Buf(f"XCB{c}") for c in range(KC)]
    bACC = [Buf(f"ACC{q}") for q in range(2)]
    bRA = [Buf(f"RA{q}") for q in range(2)]
    bA2 = [Buf(f"A2{q}") for q in range(2)]
    bIG = [Buf(f"IG{q}") for q in range(2)]
    bHH = [Buf(f"HH{q}") for q in range(2)]
    bGG = [Buf(f"GG{q}") for q in range(2)]
    bHS = [Buf(f"HS{c}", True) for c in range(KC)]
    bYA = [Buf(f"YA{c}") for c in range(KC)]
    bYB = [Buf(f"YB{c}") for c in range(KC)]
    bMG = [Buf(f"MG{c}") for c in range(KC)]
    bHM = [Buf(f"HM{f}") for f in range(NF)]
    bGU = [Buf(f"GU{q}") for q in range(2)]
    bT1 = [Buf(f"T1{q}") for q in range(2)]
    bSA = [Buf(f"SA{q}") for q in range(2)]
    bSB = [Buf(f"SB{q}") for q in range(2)]
    bSG = [Buf(f"SG{q}") for q in range(2)]
    bJUNK = Buf("JUNK")
    bRING = [Buf(f"RING{r}") for r in range(NRING)]
    bPS = [Buf(f"PS{i}") for i in range(8)]
    bCONST = Buf("CONST", True)
    bSCRA = Buf("SCRA")
    bSCRB = Buf("SCRB")
    bSETUP = Buf("SETUP", True)
    bST = {}

    def stbuf(par, name):
        key = (par, name)
        if key not in bST:
            bST[key] = Buf(f"ST{par}_{name}", True)
        return bST[key]

    def GVap(j, lo, hi):
        return HM32[:, j * D + lo: j * D + hi]

    def bGV(j):
        return [bHM[4 * j + i] for i in range(4)]

    psrot = [0]

    def ps_next():
        i = psrot[0]
        psrot[0] = (i + 1) % 8
        return i

    ring_state = {"n": 0, "pending_store": None, "since": 0}

    def wload(slot, width, scr_buf, nk=KC, col0=0):
        r = ring_state["n"] % NRING
        ring_state["n"] += 1
        src = scr_t.ap()[slot, :, 0:nk, col0:col0 + width]
        dst = RING[r][:, 0:nk, 0:width]
        sch.dma("sp", lambda e, dst=dst, src=src: e.dma_start(out=dst, in_=src), f"w{r}",
                reads=list(scr_buf), writes=[bRING[r]])
        ring_state["since"] += 1
        if ring_state["pending_store"] is not None and ring_state["since"] >= NRING:
            ring_state["pending_store"]()
            ring_state["pending_store"] = None
        if ring_state["pending_store"] is None and ring_state.get("pending_load") is not None:
            ring_state["pending_load"]()
            ring_state["pending_load"] = None
        return r

    def cst(v, k):
        return CV[:, v * 8 + k: v * 8 + k + 1]

    V_G1, V_G2, V_CW0, V_CB, V_BA, V_BX, V_LAM, V_LG = 0, 1, 2, 6, 7, 8, 9, 10
    X_C1, X_HC1, X_HBA, X_HBX = 0, 8, 16, 24

    def cx(base, k):
        return CX[:, base + k: base + k + 1]

    KSTAGE = int(os.environ.get('KSTAGE', '99'))

    def stage(n):
        sch.skip = n > KSTAGE

    stage(1)
    bZ = Buf("Z", True)

    def cdma(dst, src):
        sch.dma("sp", lambda e: e.dma_start(out=dst, in_=src), "cst", reads=[bZ], writes=[])
        bCONST.w = ("cst", sch.dmacount["cst"])

    sch.op("dve", lambda e: e.memset(VS[:], 0.0), writes=[bZ])
    sch.op("dve", lambda e: e.memset(HM32[:, 2048:4096], 0.0), writes=[bZ])
    vecs = [g1_t.ap()[0], g2_t.ap()[0], cw_t.ap()[0, 0], cw_t.ap()[0, 1], cw_t.ap()[0, 2], cw_t.ap()[0, 3],
            cb_t.ap()[0], ba_t.ap()[0].rearrange("h d -> (h d)"), bx_t.ap()[0].rearrange("h d -> (h d)"),
            lam_t.ap()[0], lg_t.ap()[0]]
    for v, ap in enumerate(vecs):
        cdma(VS[v * 8:(v + 1) * 8, :], ap.rearrange("(k p) -> k p", p=P))
    cdma(FLG[:], flag_t.ap())
    cdma(GFB[:], bass.AP(gf_t, 0, [[0, P], [1, D]]))
    WS_F = HM32[:, 0:1024].rearrange("p (g s) -> p g s", g=8)
    WST_F = HM32[:, 1024:2048].rearrange("p (g s) -> p g s", g=8)
    WA_F = HM32[:, 2048:3072].rearrange("p (g s) -> p g s", g=8)
    WX_F = HM32[:, 3072:4096].rearrange("p (g s) -> p g s", g=8)
    LB_F = HM32[:, 4096:5120]
    bSTG = Buf("STG", True)
    sdma = cdma

    sdma(WS_F, ws_t.ap()[0].rearrange("g t s -> t g s"))
    for q in range(2):
        sdma(WA_F[64 * q:64 * q + 64, :, 64 * q:64 * q + 64],
             wa_t.ap()[0].rearrange("(c q) d e -> q d c e", q=2)[q])
        sdma(WX_F[64 * q:64 * q + 64, :, 64 * q:64 * q + 64],
             wx_t.ap()[0].rearrange("(c q) d e -> q d c e", q=2)[q])
    sdma(LB_F, bass.AP(lb_t, 0, [[0, P], [1, D]]))
    BSB = X[1][:, 0, :]
    sch.dma("sp", lambda e: e.dma_start(out=BSB, in_=bass.AP(bs_t, 0, [[0, P], [1, D]])), "cst",
            reads=[], writes=[bX[1][0]])
    bCONST.w = ("cst", sch.dmacount["cst"])

    stage(2)
    sch.op("pool", lambda e: e.memset(IDF[:], 0.0), writes=[bSETUP])
    sch.op("pool", lambda e: e.affine_select(out=IDF[:], in_=IDF[:], pattern=[[-1, P]], base=0,
                                             channel_multiplier=1, compare_op=ALU.not_equal, fill=1.0),
           reads=[bSETUP], writes=[bSETUP])
    sch.op("act", lambda e: e.activation(out=IDB[:], in_=IDF[:], func=AF.Copy), reads=[bSETUP], writes=[bSETUP])
    stage(3)
    sch.op("pool", lambda e: e.affine_select(out=WS_F, in_=WS_F, pattern=[[0, 8], [-1, P]], base=0,
                                             channel_multiplier=1, compare_op=ALU.is_ge, fill=0.0),
           reads=[bCONST, bSTG], writes=[bSTG])
    stage(4)
    pi = ps_next()
    sch.op("pe", lambda e, pi=pi: e.transpose(out=PS[pi][:, 0:88], in_=VS[0:88, :], identity=IDF[0:88, 0:88]),
           reads=[bCONST, bSETUP], writes=[bPS[pi]])
    sch.op("dve", lambda e, pi=pi: e.tensor_copy(out=CV[:, 0:88], in_=PS[pi][:, 0:88]),
           reads=[bPS[pi]], writes=[bSETUP])
    stage(5)
    bCXt = Buf("CXt", True)
    sch.op("act", lambda e: e.activation(out=CX[:, 32:40], in_=CV[:, V_LAM * 8:V_LAM * 8 + 8], func=AF.Exp, scale=-1.0),
           reads=[bSETUP], writes=[bCXt])
    sch.op("act", lambda e: e.activation(out=CX[:, 40:48], in_=CX[:, 32:40], func=AF.Ln, bias=1.0),
           reads=[bCXt], writes=[bCXt])
    sch.op("dve", lambda e: e.tensor_scalar(out=CX[:, X_C1:X_C1 + 8], in0=CX[:, 40:48], scalar1=-8.0, scalar2=None,
                                            op0=ALU.mult), reads=[bCXt], writes=[bSETUP])
    sch.op("dve", lambda e: e.tensor_scalar(out=CX[:, X_HC1:X_HC1 + 8], in0=CX[:, 40:48], scalar1=-4.0, scalar2=None,
                                            op0=ALU.mult), reads=[bCXt], writes=[bSETUP])
    sch.op("dve", lambda e: e.tensor_scalar(out=CX[:, X_HBA:X_HBA + 8], in0=CV[:, V_BA * 8:V_BA * 8 + 8], scalar1=0.5,
                                            scalar2=None, op0=ALU.mult), reads=[bSETUP], writes=[bSETUP])
    sch.op("dve", lambda e: e.tensor_scalar(out=CX[:, X_HBX:X_HBX + 8], in0=CV[:, V_BX * 8:V_BX * 8 + 8], scalar1=0.5,
                                            scalar2=None, op0=ALU.mult), reads=[bSETUP], writes=[bSETUP])
    stage(6)
    sch.op("act", lambda e: e.activation(out=WAB[:], in_=WA_F, func=AF.Copy), reads=[bSTG, bCONST], writes=[bSETUP])
    sch.op("act", lambda e: e.activation(out=WXB[:], in_=WX_F, func=AF.Copy), reads=[bSTG, bCONST], writes=[bSETUP])
    stage(7)
    for g in range(8):
        pi = ps_next()
        sch.op("pe", lambda e, pi=pi, g=g: e.transpose(out=PS[pi][:, 0:P], in_=WS_F[:, g, :], identity=IDF[:]),
               reads=[bSTG, bSETUP], writes=[bPS[pi]])
        if 's7a' in DBG:
            continue
        sch.op("dve", lambda e, pi=pi, g=g: e.tensor_copy(out=WST_F[:, g, :], in_=PS[pi][:, 0:P]),
               reads=[bPS[pi]], writes=[bSTG])
    if 's7b' not in DBG:
        sch.op("act", lambda e: e.activation(out=WST[:], in_=WST_F, func=AF.Copy), reads=[bSTG], writes=[bSETUP])
    stage(8)
    for g in range(8):
        pi = ps_next()

        def mm(e, pi=pi, g=g):
            return e.matmul(PS[pi][:, 0:P], lhsT=LB_F[:, g * P:(g + 1) * P], rhs=WST_F[:, g, :], start=True, stop=True)

        sch.op("pe", mm, reads=[bSTG, bSETUP, bCONST], writes=[bPS[pi]])
        sch.op("dve", lambda e, pi=pi, g=g: e.tensor_tensor(out=CBI[:, g, :], in0=PS[pi][:, 0:P],
                                                            in1=BSB[:, g * P:(g + 1) * P], op=ALU.add),
               reads=[bPS[pi], bCONST, bX[1][0]], writes=[bSETUP])
    stage(0)
    sch.op("dve", lambda e: e.memset(HS[:], 0.0), writes=bHS)
    sch.op("dve", lambda e: e.memset(HALO[:], 0.0), writes=bHALO)

    cvk = [0]
    NO_CONV = 'noconv' in DBG
    bCVs = [Buf(f"cv{i}") for i in range(4)]

    def cast_dma(slot, col0, src_ap_rows_cols):
        kc = src_ap_rows_cols.shape[0] // P
        w = src_ap_rows_cols.shape[1]
        src = src_ap_rows_cols.rearrange("(kc p) n -> p kc n", p=P)
        dst = scr_t.ap()[slot, :, 0:kc, col0:col0 + w]
        k = cvk[0]
        cvk[0] += 1
        if NO_CONV:
            return
        sch.dma("pool", lambda e: e.dma_start(out=dst, in_=src), f"cv{k % 4}", reads=[], writes=[bCVs[k % 4]])

    win = win_t.ap()[0]
    wgu = wgu_t.ap()[0]
    wdn = wd_t.ap()[0]
    for c in range(8):
        cast_dma(S_A + c, 0, win[:, c * P:(c + 1) * P])
    bSCRA_l = []
    for i in range(4):
        fb = Buf(f"scra{i}")
        fb.w = bCVs[i].w
        bSCRA_l.append(fb)
    for c in range(8):
        cast_dma(S_A + c, P, win[:, D + c * P: D + (c + 1) * P])
    for h in range(2):
        cast_dma(S_V + h, 0, win[:, 3 * D + h * 512: 3 * D + (h + 1) * 512])
    for h in range(2):
        cast_dma(S_U + h, 0, win[:, 2 * D + h * 512: 2 * D + (h + 1) * 512])
    for m in range(8):
        cast_dma(S_M + m, 0, wpa_t.ap()[0][:, m * P:(m + 1) * P])
        cast_dma(S_M + m, P, wpb_t.ap()[0][:, m * P:(m + 1) * P])
        cast_dma(S_M + m, 2 * P, win[:, 4 * D + m * P: 4 * D + (m + 1) * P])
        cast_dma(S_M + m, 3 * P, win[:, 5 * D + m * P: 5 * D + (m + 1) * P])
    for h in range(2):
        cast_dma(S_O + h, 0, wo_t.ap()[0][:, h * 512:(h + 1) * 512])
    for s in range(11):
        cast_dma(S_F + s, 0, wgu[:, s * 256:(s + 1) * 256])
        cast_dma(S_F + s, 256, wgu[:, DFF + s * 256: DFF + (s + 1) * 256])
    for h in range(2):
        for q in range(3):
            f0 = 8 * q
            f1 = min(8 * q + 8, NF)
            cast_dma(S_D + 3 * h + q, 0, wdn[f0 * P: f1 * P, h * 512:(h + 1) * 512])
    bSCRB_l = []
    for i in range(4):
        fb = Buf(f"scrb{i}")
        fb.w = bCVs[i].w
        bSCRB_l.append(fb)


    tile_ctr = [0]

    def load_x(b, src_t, t0, eng="sp"):
        src = src_t.ap()[t0:t0 + T, :].rearrange("(j p) d -> p j d", p=P)
        sem = f"x{b}" if eng == "sp" else f"xs{b}"
        sch.dma(eng, lambda e: e.dma_start(out=X[b][:], in_=src), sem, reads=[], writes=bX[b])

    def rstd_from_ssq(par, name_ssq, name_out, col_ssq, col_out, n):
        st = ST[par]
        sch.op("dve", lambda e: e.tensor_scalar(out=st[:, col_out:col_out + n], in0=st[:, col_ssq:col_ssq + n],
                                                scalar1=1.0 / D, scalar2=EPS, op0=ALU.mult, op1=ALU.add),
               reads=[stbuf(par, name_ssq)], writes=[stbuf(par, name_out)])
        sch.op("act", lambda e: e.activation(out=st[:, col_out:col_out + n], in_=st[:, col_out:col_out + n], func=AF.Sqrt),
               reads=[stbuf(par, name_out)], writes=[stbuf(par, name_out)])
        sch.op("dve", lambda e: e.reciprocal(out=st[:, col_out:col_out + n], in_=st[:, col_out:col_out + n]),
               reads=[stbuf(par, name_out)], writes=[stbuf(par, name_out)])

    def norm_part(b, par, tag, c0):
        st = ST[par]
        for j in range(NJ):
            sch.op("act", lambda e, j=j: e.activation(out=JUNK[:], in_=X[b][:, j, :], func=AF.Square,
                                                     accum_out=st[:, c0 + j:c0 + j + 1]),
                   reads=[bX[b][j]], writes=[stbuf(par, tag + "ssq")])
        rstd_from_ssq(par, tag + "ssq", tag + "rstd", c0, c0 + 4, 4)
        for j in range(NJ):
            sch.op("act", lambda e, j=j: e.activation(out=TMB[:, j, :], in_=X[b][:, j, :], func=AF.Copy,
                                                     scale=st[:, c0 + 4 + j:c0 + 5 + j]),
                   reads=[bX[b][j], stbuf(par, tag + "rstd")], writes=[bTMB[j]])

    def transpose_part(xtb, gvec):
        for k in range(KC):
            pi = ps_next()

            def tr(e, pi=pi, k=k):
                last = None
                for j in range(NJ):
                    last = e.transpose(out=PSB[pi][:, j * P:(j + 1) * P], in_=TMB[:, j, k * P:(k + 1) * P],
                                       identity=IDB[:])
                return last

            sch.op("pe", tr, reads=bTMB + [bSETUP], writes=[bPS[pi]])
            sch.op("act", lambda e, pi=pi, k=k: e.activation(out=XT[xtb][:, k, :], in_=PSB[pi][:, 0:T], func=AF.Copy,
                                                            scale=cst(gvec, k)),
                   reads=[bPS[pi], bSETUP], writes=[bXT[xtb][k]])

    def norm_and_transpose(b, par, xtb, gvec, tag, c0):
        norm_part(b, par, tag, c0)
        transpose_part(xtb, gvec)

    def fm_matmul(pi, r, col0, rhs_t, rhs_bufs):
        def mm(e):
            last = None
            for k in range(KC):
                last = e.matmul(PS[pi][:, :], lhsT=RING[r][:, k, col0:col0 + P], rhs=rhs_t[:, k, :],
                                start=(k == 0), stop=(k == KC - 1))
            return last
        sch.op("pe", mm, reads=[bRING[r], bSETUP] + rhs_bufs, writes=[bPS[pi]])

    RA4 = [RA[0], RA[1], SA[0], SA[1]]
    bRA4 = [bRA[0], bRA[1], bSA[0], bSA[1]]
    A24 = [A2[0], A2[1], SB_[0], SB_[1]]
    bA24 = [bA2[0], bA2[1], bSB[0], bSB[1]]
    IG4 = [IG[0], IG[1], T1[0], T1[1]]
    bIG4 = [bIG[0], bIG[1], bT1[0], bT1[1]]

    def branch_a(xtb, main, scr_buf, mid_hook=None):
        tt_eng = "pool" if (main and USE_POOL) else "dve"
        rr = {}

        def ph1(cs):
            for c in cs:
                q = c % 4
                qa = c % 2
                r = wload(S_A + c, P, scr_buf)
                p1 = ps_next()
                fm_matmul(p1, r, 0, XT[xtb], bXT[xtb])
                sch.op("act", lambda e, c=c, q=q: e.activation(out=RXT[q][:, 0:3], in_=HALO[:, c, 0:3], func=AF.Copy),
                       reads=[bHALO[c]], writes=[bRXT[q]])
                sch.op("act", lambda e, q=q, p1=p1: e.activation(out=RXT[q][:, 3:515], in_=PS[p1][:, :], func=AF.Copy),
                       reads=[bPS[p1]], writes=[bRXT[q]])
                sch.op("act", lambda e, c=c, q=q: e.activation(out=HALO[:, c, 0:3], in_=RXT[q][:, 512:515], func=AF.Copy),
                       reads=[bRXT[q]], writes=[bHALO[c]])
                sch.op("dve", lambda e, c=c, q=q, qa=qa: e.tensor_scalar(out=ACC[qa][:], in0=RXT[q][:, 0:512],
                                                                         scalar1=cst(V_CW0 + 0, c), scalar2=cst(V_CB, c),
                                                                         op0=ALU.mult, op1=ALU.add),
                       reads=[bRXT[q], bSETUP], writes=[bACC[qa]])
                for kk in (1, 2):
                    sch.op("dve", lambda e, c=c, q=q, qa=qa, kk=kk: e.scalar_tensor_tensor(
                        out=ACC[qa][:], in0=RXT[q][:, kk:kk + 512], scalar=cst(V_CW0 + kk, c), in1=ACC[qa][:],
                        op0=ALU.mult, op1=ALU.add), reads=[bRXT[q], bACC[qa], bSETUP], writes=[bACC[qa]])
                sch.op("dve", lambda e, c=c, q=q, qa=qa: e.scalar_tensor_tensor(
                    out=XCB[:, c, :], in0=RXT[q][:, 3:515], scalar=cst(V_CW0 + 3, c), in1=ACC[qa][:],
                    op0=ALU.mult, op1=ALU.add), reads=[bRXT[q], bACC[qa], bSETUP], writes=[bXCB[c]])

        def ph24(cs):
            for c in cs:
                q = c % 4
                p2 = ps_next()
                p3 = ps_next()
                sch.op("pe", lambda e, c=c, p2=p2: e.matmul(PS[p2][:, :], lhsT=WAB[:, c, :], rhs=XCB[:, c, :],
                                                            start=True, stop=True),
                       reads=[bXCB[c], bSETUP], writes=[bPS[p2]])
                sch.op("pe", lambda e, c=c, p3=p3: e.matmul(PS[p3][:, :], lhsT=WXB[:, c, :], rhs=XCB[:, c, :],
                                                            start=True, stop=True),
                       reads=[bXCB[c], bSETUP], writes=[bPS[p3]])
                sch.op("act", lambda e, c=c, q=q, p2=p2: e.activation(out=RA4[q][:], in_=PS[p2][:, :], func=AF.Tanh,
                                                                     scale=0.5, bias=cx(X_HBA, c)),
                       reads=[bPS[p2], bSETUP], writes=[bRA4[q]])
                sch.op("act", lambda e, c=c, q=q, p3=p3: e.activation(out=IG4[q][:], in_=PS[p3][:, :], func=AF.Tanh,
                                                                     scale=0.5, bias=cx(X_HBX, c)),
                       reads=[bPS[p3], bSETUP], writes=[bIG4[q]])
            for c in cs:
                q = c % 4
                sch.op("act", lambda e, c=c, q=q: e.activation(out=A24[q][:], in_=RA4[q][:], func=AF.Exp,
                                                              scale=cx(X_C1, c), bias=cx(X_C1, c)),
                       reads=[bRA4[q], bSETUP], writes=[bA24[q]])
                sch.op("act", lambda e, c=c, q=q: e.activation(out=RA4[q][:], in_=RA4[q][:], func=AF.Exp,
                                                              scale=cx(X_HC1, c), bias=cx(X_HC1, c)),
                       reads=[bRA4[q], bSETUP], writes=[bRA4[q]])
                sch.op("dve", lambda e, c=c, q=q: e.scalar_tensor_tensor(out=IG4[q][:], in0=IG4[q][:], scalar=1.0,
                                                                         in1=XCB[:, c, :], op0=ALU.add, op1=ALU.mult),
                       reads=[bIG4[q], bXCB[c]], writes=[bIG4[q]])
            for c in cs:
                q = c % 4
                sch.op("act", lambda e, q=q: e.activation(out=A24[q][:], in_=A24[q][:], func=AF.Sqrt, scale=-1.0, bias=1.0),
                       reads=[bA24[q]], writes=[bA24[q]])

        def ph5(cs):
            for c in cs:
                q = c % 4
                hq = c % 2
                sch.op("dve", lambda e, q=q: e.scalar_tensor_tensor(out=IG4[q][:], in0=IG4[q][:], scalar=0.5,
                                                                    in1=A24[q][:], op0=ALU.mult, op1=ALU.mult),
                       reads=[bIG4[q], bA24[q]], writes=[bIG4[q]])
                sch.op("dve", lambda e, c=c, q=q, hq=hq: e.tensor_tensor_scan(
                    out=HH[hq][:], data0=RA4[q][:], data1=IG4[q][:], initial=HS[:, c:c + 1], op0=ALU.mult, op1=ALU.add),
                    reads=[bRA4[q], bIG4[q], bHS[c]], writes=[bHH[hq]])
                sch.op("dve", lambda e, c=c, hq=hq: e.tensor_copy(out=HS[:, c:c + 1], in_=HH[hq][:, T - 1:T]),
                       reads=[bHH[hq]], writes=[bHS[c]])
                if main:
                    p4 = ps_next()
                    r4 = wload(S_A + c, P, scr_buf, col0=P)
                    fm_matmul(p4, r4, 0, XT[xtb], bXT[xtb])
                    sch.op("act", lambda e, hq=hq, p4=p4: e.activation(out=GG[hq][:], in_=PS[p4][:, :],
                                                                      func=AF.Gelu_apprx_tanh),
                           reads=[bPS[p4]], writes=[bGG[hq]])
                    sch.op(tt_eng, lambda e, c=c, hq=hq: e.tensor_tensor(out=YA[:, c, :], in0=GG[hq][:], in1=HH[hq][:],
                                                                         op=ALU.mult),
                           reads=[bGG[hq], bHH[hq]], writes=[bYA[c]])

        g0, g1 = [0, 1, 2, 3], [4, 5, 6, 7]
        ph1(g0)
        ph1(g1)
        ph24(g0)
        if mid_hook is not None:
            mid_hook()
        ph5(g0)
        ph24(g1)
        ph5(g1)

    def branch_b_v(xtb, par):
        st = ST[par]
        C_VS, C_VQ, C_MS, C_MEAN, C_M2, C_VAR, C_RSTD, C_NB = 16, 24, 28, 32, 36, 40, 44, 48
        for h in range(2):
            r = wload(S_V + h, 512, bSCRB_l)
            for j in range(NJ):
                pi = ps_next()

                def mm(e, pi=pi, r=r, j=j):
                    last = None
                    for k in range(KC):
                        last = e.matmul(PS[pi][:, :], lhsT=XT[xtb][:, k, j * P:(j + 1) * P], rhs=RING[r][:, k, :],
                                        start=(k == 0), stop=(k == KC - 1))
                    return last
                sch.op("pe", mm, reads=[bRING[r]] + bXT[xtb], writes=[bPS[pi]])
                sch.op("act", lambda e, pi=pi, j=j, h=h: e.activation(
                    out=GVap(j, h * 512, (h + 1) * 512), in_=PS[pi][:, :], func=AF.Gelu_apprx_tanh,
                    accum_out=st[:, C_VS + 2 * j + h:C_VS + 2 * j + h + 1]),
                    reads=[bPS[pi]], writes=[bGV(j)[2 * h], bGV(j)[2 * h + 1], stbuf(par, "vs")])
        for j in range(NJ):
            sch.op("act", lambda e, j=j: e.activation(out=JUNK[:], in_=GVap(j, 0, D), func=AF.Square,
                                                     accum_out=st[:, C_VQ + j:C_VQ + j + 1]),
                   reads=bGV(j), writes=[stbuf(par, "vq")])
        vs3 = st[:, C_VS:C_VS + 8].rearrange("p (j h) -> p j h", h=2)
        sch.op("dve", lambda e: e.tensor_tensor(out=st[:, C_MS:C_MS + 4], in0=vs3[:, :, 0], in1=vs3[:, :, 1], op=ALU.add),
               reads=[stbuf(par, "vs")], writes=[stbuf(par, "ms")])
        sch.op("dve", lambda e: e.tensor_scalar(out=st[:, C_MEAN:C_MEAN + 4], in0=st[:, C_MS:C_MS + 4], scalar1=1.0 / D,
                                                scalar2=None, op0=ALU.mult),
               reads=[stbuf(par, "ms")], writes=[stbuf(par, "mean")])
        sch.op("dve", lambda e: e.tensor_tensor(out=st[:, C_M2:C_M2 + 4], in0=st[:, C_MEAN:C_MEAN + 4],
                                                in1=st[:, C_MEAN:C_MEAN + 4], op=ALU.mult),
               reads=[stbuf(par, "mean")], writes=[stbuf(par, "m2")])
        sch.op("dve", lambda e: e.scalar_tensor_tensor(out=st[:, C_VAR:C_VAR + 4], in0=st[:, C_VQ:C_VQ + 4], scalar=1.0 / D,
                                                       in1=st[:, C_M2:C_M2 + 4], op0=ALU.mult, op1=ALU.subtract),
               reads=[stbuf(par, "vq"), stbuf(par, "m2")], writes=[stbuf(par, "var")])
        sch.op("dve", lambda e: e.tensor_scalar(out=st[:, C_RSTD:C_RSTD + 4], in0=st[:, C_VAR:C_VAR + 4], scalar1=EPS,
                                                scalar2=None, op0=ALU.add),
               reads=[stbuf(par, "var")], writes=[stbuf(par, "vrstd")])
        sch.op("act", lambda e: e.activation(out=st[:, C_RSTD:C_RSTD + 4], in_=st[:, C_RSTD:C_RSTD + 4], func=AF.Sqrt),
               reads=[stbuf(par, "vrstd")], writes=[stbuf(par, "vrstd")])
        sch.op("dve", lambda e: e.reciprocal(out=st[:, C_RSTD:C_RSTD + 4], in_=st[:, C_RSTD:C_RSTD + 4]),
               reads=[stbuf(par, "vrstd")], writes=[stbuf(par, "vrstd")])
        sch.op("dve", lambda e: e.scalar_tensor_tensor(out=st[:, C_NB:C_NB + 4], in0=st[:, C_MEAN:C_MEAN + 4], scalar=-1.0,
                                                       in1=st[:, C_RSTD:C_RSTD + 4], op0=ALU.mult, op1=ALU.mult),
               reads=[stbuf(par, "mean"), stbuf(par, "vrstd")], writes=[stbuf(par, "nb")])
        for j in range(NJ):
            sch.op("act", lambda e, j=j: e.activation(out=TMB[:, j, :], in_=GVap(j, 0, D), func=AF.Identity,
                                                     scale=st[:, C_RSTD + j:C_RSTD + j + 1],
                                                     bias=st[:, C_NB + j:C_NB + j + 1]),
                   reads=bGV(j) + [stbuf(par, "vrstd"), stbuf(par, "nb")], writes=[bTMB[j]])

    def branch_b_mix(xtb):
        ru = None
        for g in range(8):
            q = g % 2
            if g % 4 == 0:
                ru = wload(S_U + g // 4, 512, bSCRB_l)
            pm = ps_next()

            def mix(e, pm=pm, g=g):
                last = None
                for j in range(NJ):
                    last = e.matmul(PS[pm][:, j * P:(j + 1) * P], lhsT=TMB[:, j, g * P:(g + 1) * P], rhs=WST[:, g, :],
                                    start=True, stop=True)
                return last
            sch.op("pe", mix, reads=bTMB + [bSETUP], writes=[bPS[pm]])
            pu = ps_next()
            fm_matmul(pu, ru, (g % 4) * P, XT[xtb], bXT[xtb])
            sch.op("act", lambda e, q=q, pu=pu: e.activation(out=GU[q][:], in_=PS[pu][:, :], func=AF.Gelu_apprx_tanh),
                   reads=[bPS[pu]], writes=[bGU[q]])
            cb_ap = bass.AP(CBI, g * P, [[8 * P, P], [0, NJ], [1, P]])
            sch.op("dve", lambda e, q=q, pm=pm, g=g, cb_ap=cb_ap: e.scalar_tensor_tensor(
                out=T1[q][:].rearrange("p (j t) -> p j t", j=NJ), in0=PS[pm][:, :].rearrange("p (j t) -> p j t", j=NJ),
                scalar=cst(V_LG, g), in1=cb_ap, op0=ALU.mult, op1=ALU.add),
                reads=[bPS[pm], bSETUP], writes=[bT1[q]])
            sch.op("pool" if USE_POOL else "dve",
                   lambda e, q=q, g=g: e.tensor_tensor(out=YB[:, g, :], in0=T1[q][:], in1=GU[q][:], op=ALU.mult),
                   reads=[bT1[q], bGU[q]], writes=[bYB[g]])

    def merge(xtb):
        for m in range(8):
            q = m % 2
            r = wload(S_M + m, 512, bSCRB_l)
            pa, pb, pga, pgb = ps_next(), ps_next(), ps_next(), ps_next()
            fm_matmul(pga, r, 2 * P, XT[xtb], bXT[xtb])
            fm_matmul(pgb, r, 3 * P, XT[xtb], bXT[xtb])
            fm_matmul(pa, r, 0, YA, bYA)
            fm_matmul(pb, r, P, YB, bYB)
            sch.op("act", lambda e, q=q, pga=pga: e.activation(out=SA[q][:], in_=PS[pga][:, :], func=AF.Tanh, scale=0.5),
                   reads=[bPS[pga]], writes=[bSA[q]])
            sch.op("act", lambda e, q=q, pgb=pgb: e.activation(out=SB_[q][:], in_=PS[pgb][:, :], func=AF.Tanh, scale=0.5),
                   reads=[bPS[pgb]], writes=[bSB[q]])
            sch.op("dve", lambda e, q=q, pa=pa: e.scalar_tensor_tensor(out=SA[q][:], in0=SA[q][:], scalar=1.0,
                                                                       in1=PS[pa][:, :], op0=ALU.add, op1=ALU.mult),
                   reads=[bSA[q], bPS[pa]], writes=[bSA[q]])
            sch.op("dve", lambda e, q=q, pb=pb: e.scalar_tensor_tensor(out=SB_[q][:], in0=SB_[q][:], scalar=1.0,
                                                                       in1=PS[pb][:, :], op0=ALU.add, op1=ALU.mult),
                   reads=[bSB[q], bPS[pb]], writes=[bSB[q]])
            sch.op("pool" if USE_POOL else "dve",
                   lambda e, q=q, m=m: e.tensor_tensor(out=MG[:, m, :], in0=SA[q][:], in1=SB_[q][:], op=ALU.add),
                   reads=[bSA[q], bSB[q]], writes=[bMG[m]])

    def out_proj(b):
        for h in range(2):
            r = wload(S_O + h, 512, bSCRB_l)
            for j in range(NJ):
                pi = ps_next()

                def mm(e, pi=pi, r=r, j=j):
                    last = None
                    for k in range(KC):
                        last = e.matmul(PS[pi][:, :], lhsT=MG[:, k, j * P:(j + 1) * P], rhs=RING[r][:, k, :],
                                        start=(k == 0), stop=(k == KC - 1))
                    return last
                sch.op("pe", mm, reads=[bRING[r]] + bMG, writes=[bPS[pi]])
                sch.op("dve", lambda e, pi=pi, j=j, h=h: e.scalar_tensor_tensor(
                    out=X[b][:, j, h * 512:(h + 1) * 512], in0=PS[pi][:, :], scalar=0.5,
                    in1=X[b][:, j, h * 512:(h + 1) * 512], op0=ALU.mult, op1=ALU.add),
                    reads=[bPS[pi], bX[b][j]], writes=[bX[b][j]])

    def ffn(b, xtb, hook_a=None, hook_b=None):
        for s in range(11):
            r = wload(S_F + s, 512, bSCRB_l)
            for e2 in range(2):
                f = 2 * s + e2
                q = f % 2
                pg, pu = ps_next(), ps_next()
                fm_matmul(pg, r, e2 * P, XT[xtb], bXT[xtb])
                fm_matmul(pu, r, 256 + e2 * P, XT[xtb], bXT[xtb])
                sch.op("act", lambda e, q=q, pg=pg: e.activation(out=SG[q][:], in_=PS[pg][:, :], func=AF.Silu),
                       reads=[bPS[pg]], writes=[bSG[q]])
                sch.op("dve", lambda e, q=q, pu=pu, f=f: e.tensor_tensor(out=HM[:, f * T:(f + 1) * T], in0=PS[pu][:, :], in1=SG[q][:],
                                                                        op=ALU.mult),
                       reads=[bPS[pu], bSG[q]], writes=[bHM[f]])
        if hook_a is not None:
            hook_a()
        for h in range(2):
            pj = [ps_next() for _ in range(NJ)]
            for qq in range(3):
                nfl = min(8, NF - 8 * qq)
                r = wload(S_D + 3 * h + qq, 512, bSCRB_l, nfl)

                def mm(e, r=r, qq=qq, nfl=nfl, pj=pj):
                    last = None
                    for fl in range(nfl):
                        f = 8 * qq + fl
                        for j in range(NJ):
                            last = e.matmul(PS[pj[j]][:, :], lhsT=HM[:, f * T + j * P:f * T + (j + 1) * P], rhs=RING[r][:, fl, :],
                                            start=(f == 0), stop=(f == NF - 1))
                    return last
                sch.op("pe", mm, reads=[bRING[r]] + bHM[8 * qq:8 * qq + nfl], writes=[bPS[i] for i in pj])
            for j in range(NJ):
                sch.op("dve", lambda e, j=j, h=h, pj=pj: e.tensor_tensor(
                    out=X[b][:, j, h * 512:(h + 1) * 512], in0=PS[pj[j]][:, :], in1=X[b][:, j, h * 512:(h + 1) * 512],
                    op=ALU.add), reads=[bPS[pj[j]], bX[b][j]], writes=[bX[b][j]])

        if hook_b is not None:
            hook_b()

    def final_norm_store(b, par, t0):
        st = ST[par]
        c0 = 56
        for j in range(NJ):
            sch.op("act", lambda e, j=j: e.activation(out=JUNK[:], in_=X[b][:, j, :], func=AF.Square,
                                                     accum_out=st[:, c0 + j:c0 + j + 1]),
                   reads=[bX[b][j]], writes=[stbuf(par, "fssq")])
        rstd_from_ssq(par, "fssq", "frstd", c0, c0 + 4, 4)
        for j in range(NJ):
            sch.op("dve", lambda e, j=j: e.scalar_tensor_tensor(out=X[b][:, j, :], in0=X[b][:, j, :],
                                                                scalar=st[:, c0 + 4 + j:c0 + 5 + j], in1=GFB[:],
                                                                op0=ALU.mult, op1=ALU.mult),
                   reads=[bX[b][j], stbuf(par, "frstd"), bCONST], writes=[bX[b][j]])
        dst = out_t.ap()[t0:t0 + T, :].rearrange("(j p) d -> p j d", p=P)

        if IO_ON_POOL:
            sch.dma("pool", lambda e: e.dma_start(out=dst, in_=X[b][:]), f"o{b}", reads=bX[b], writes=[])
        else:
            def do_store():
                sch.dma("sp", lambda e: e.dma_start(out=dst, in_=X[b][:]), f"o{b}", reads=bX[b], writes=[])
            ring_state["pending_store"] = do_store
            ring_state["since"] = 0

    seq = [("pre", i) for i in range(n_pre)] + [("main", i) for i in range(n_main)]
    if 'setuponly' in DBG:
        seq = []
        load_x(0, xm_t, 0)
        dst0 = out_t.ap()[0:T, :].rearrange("(j p) d -> p j d", p=P)
        sch.op("dve", lambda e: e.tensor_copy(out=X[1][:, 0, 0:P], in_=CBI[:, 7, :]), reads=[bSETUP], writes=bX[1])
        sch.dma("sp", lambda e: e.dma_start(out=dst0, in_=X[0][:]), "o0", reads=bX[0] + bX[1], writes=[])
    if seq:
        kind, i = seq[0]
        load_x(0, xp_t if kind == "pre" else xm_t, i * T)
    if seq:
        norm_part(0, 0, "n1", 0)
        transpose_part(0, V_G1)
    for n, (kind, i) in enumerate(seq):
        b = n % 2
        par = n % 2
        has_next = n + 1 < len(seq)
        nb, npar = (n + 1) % 2, (n + 1) % 2
        if has_next:
            k2, i2 = seq[n + 1]

            def do_load(n=n, k2=k2, i2=i2, kind=kind):
                load_x((n + 1) % 2, xp_t if k2 == "pre" else xm_t, i2 * T,
                       eng="pool" if (IO_ON_POOL and kind == "main") else "sp")
            if ring_state["pending_store"] is None:
                do_load()
            else:
                ring_state["pending_load"] = do_load

        def next_norm(nb=nb, npar=npar):
            if ring_state.get("pending_load") is not None:
                if ring_state["pending_store"] is not None:
                    ring_state["pending_store"]()
                    ring_state["pending_store"] = None
                ring_state["pending_load"]()
                ring_state["pending_load"] = None
            norm_part(nb, npar, "n1", 0)

        def next_transpose():
            transpose_part(0, V_G1)

        if kind == "pre":
            branch_a(0, False, bSCRA_l)
            if i == n_pre - 1:
                sch.op("dve", lambda e: e.tensor_scalar(out=HS[:], in0=HS[:], scalar1=FLG[:, 0:1], scalar2=None,
                                                        op0=ALU.mult), reads=bHS + [bCONST], writes=bHS)
            if has_next:
                next_norm()
                next_transpose()
        else:
            if HOIST_V:
                branch_a(0, True, bSCRB_l, mid_hook=lambda par=par: branch_b_v(0, par))
            else:
                branch_a(0, True, bSCRB_l)
                branch_b_v(0, par)
            branch_b_mix(0)
            merge(0)
            out_proj(b)
            norm_and_transpose(b, par, 1, V_G2, "n2", 8)
            if PIPE_NEXT:
                ffn(b, 1, hook_a=next_norm if has_next else None, hook_b=next_transpose if has_next else None)
                final_norm_store(b, par, i * T)
            else:
                ffn(b, 1)
                final_norm_store(b, par, i * T)
                if has_next:
                    next_norm()
                    next_transpose()
    if ring_state["pending_store"] is not None:
        ring_state["pending_store"]()
        ring_state["pending_store"] = None
    final_waits = [(k, v) for k, v in sch.dmacount.items() if k.startswith("o")]

    semnames = ["pe", "act", "dve", "pool"] + sorted(sch.dmacount.keys())
    import contextlib
    with contextlib.ExitStack() as es:
        for s in semnames:
            sems[s] = es.enter_context(nc.semaphore(f"s_{s}"))
        block = es.enter_context(nc.Block())

        def run(engobj, name):
            for waits, emit, (semname, inc) in sch.streams[name]:
                for k, v in waits:
                    engobj.wait_ge(sems[k], v)
                inst = emit(engobj)
                inst.then_inc(sems[semname], inc)

        @block.tensor
        def _(e):
            run(e, "pe")

        @block.scalar
        def _(e):
            run(e, "act")

        @block.vector
        def _(e):
            run(e, "dve")

        @block.gpsimd
        def _(e):
            run(e, "pool")

        @block.sync
        def _(e):
            run(e, "sp")
            for k, v in final_waits:
                e.wait_ge(sems[k], v)
    return nc


WEIGHT_NAMES = ["norm_mix_g", "w_in", "conv_w", "conv_b", "rg_wa", "rg_ba", "rg_wx", "rg_bx", "rg_lambda",
                "sgu_ln_g", "sgu_ln_b", "sgu_ws", "sgu_bs", "w_proj_a", "w_proj_b", "w_out", "norm_ffn_g",
                "w_gate_up", "w_down", "norm_final_g"]


def run_sharded(inputs, n_main, n_pre, trace=False):
    x = np.ascontiguousarray(np.asarray(inputs["x"], dtype=np.float32))
    B, S, _ = x.shape
    half = S // 2
    assert half == n_main * T and (n_pre == 0 or half == n_pre * T)
    weights = {k: np.ascontiguousarray(np.asarray(inputs[k], dtype=np.float32)) for k in WEIGHT_NAMES}
    nc = build_nc(n_main, n_pre)
    in_maps = []
    for b in range(B):
        for hh in range(2):
            m = dict(weights)
            m["xm"] = np.ascontiguousarray(x[b, hh * half:(hh + 1) * half])
            if hh == 0:
                m["xp"] = np.zeros((max(n_pre, 1) * T, D), np.float32)
                m["flag"] = np.zeros((P, 1), np.float32)
            else:
                m["xp"] = np.ascontiguousarray(x[b, 0:half])
                m["flag"] = np.ones((P, 1), np.float32)
            in_maps.append(m)
    res = run_bass_kernel_spmd(nc, in_maps, core_ids=list(range(2 * B)), trace=trace)
    out = np.empty((B, S, D), np.float32)
    for b in range(B):
        for hh in range(2):
            out[b, hh * half:(hh + 1) * half] = res.results[2 * b + hh]["out"]
    return out, res


def kernel(**inputs):
    out, _ = run_sharded(inputs, 8, 8)
    return out
```

```python
import numpy as np
import concourse.bass as bass
import concourse.mybir as mybir
from concourse.bass_utils import run_bass_kernel_spmd

F32 = mybir.dt.float32
BF16 = mybir.dt.bfloat16
AF = mybir.ActivationFunctionType
ALU = mybir.AluOpType

P = 128
D = 1024
KC = 8
T = 512
NJ = 4
DFF = 2816
NF = 22
EPS = 1e-6
SAME_ENG_SYNC = True
RELAX_SAME_ENG = False
USE_POOL = True
HOIST_V = True
PIPE_NEXT = False
IO_ON_POOL = True
NRING = 4

ENGS = ["pe", "act", "dve", "pool", "sp"]


class Buf:
    __slots__ = ("name", "w", "r", "small")

    def __init__(self, name, small=False):
        self.name = name
        self.w = None
        self.r = {}
        self.small = small


class Sched:
    def __init__(self):
        self.streams = {e: [] for e in ENGS}
        self.count = {e: 0 for e in ENGS}
        self.seen = {e: {} for e in ENGS}
        self.dmacount = {}
        self.skip = False

    def _deps(self, eng, reads, writes):
        deps = {}

        def add(d, small):
            if d is None:
                return
            k, v = d
            if k == eng and not small and RELAX_SAME_ENG:
                return
            if deps.get(k, 0) < v:
                deps[k] = v

        for b in reads:
            add(b.w, b.small)
        for b in writes:
            add(b.w, b.small)
            for k, v in b.r.items():
                add((k, v), b.small)
        waits = []
        for k, v in deps.items():
            if k == eng and (eng == "pe" or not SAME_ENG_SYNC):
                continue
            if self.seen[eng].get(k, 0) >= v:
                continue
            self.seen[eng][k] = v
            waits.append((k, v))
        return waits

    def op(self, eng, emit, reads=(), writes=()):
        if self.skip:
            return
        waits = self._deps(eng, reads, writes)
        self.count[eng] += 1
        v = self.count[eng]
        self.streams[eng].append((waits, emit, (eng, 1)))
        for b in reads:
            if b.r.get(eng, 0) < v:
                b.r[eng] = v
        for b in writes:
            b.w = (eng, v)
            b.r = {}

    def dma(self, eng, emit, sem, reads=(), writes=()):
        if self.skip:
            self.dmacount.setdefault(sem, 0)
            return
        waits = self._deps(eng, reads, writes)
        self.dmacount[sem] = self.dmacount.get(sem, 0) + 16
        v = self.dmacount[sem]
        self.streams[eng].append((waits, emit, (sem, 16)))
        for b in reads:
            if b.r.get(sem, 0) < v:
                b.r[sem] = v
        for b in writes:
            b.w = (sem, v)
            b.r = {}


def build_nc(n_main, n_pre):
    import os
    DBG = os.environ.get('KDBG', '')
    nc = bass.Bass("TRN2", target_bir_lowering=False)
    NTOK = n_main * T
    NPTOK = max(n_pre, 1) * T

    def din(name, shape):
        return nc.dram_tensor(name, list(shape), F32, kind="ExternalInput")

    xm_t = din("xm", [NTOK, D])
    xp_t = din("xp", [NPTOK, D])
    flag_t = din("flag", [P, 1])
    g1_t = din("norm_mix_g", [1, D])
    win_t = din("w_in", [1, D, 6 * D])
    cw_t = din("conv_w", [1, 4, D])
    cb_t = din("conv_b", [1, D])
    wa_t = din("rg_wa", [1, 16, 64, 64])
    ba_t = din("rg_ba", [1, 16, 64])
    wx_t = din("rg_wx", [1, 16, 64, 64])
    bx_t = din("rg_bx", [1, 16, 64])
    lam_t = din("rg_lambda", [1, D])
    lg_t = din("sgu_ln_g", [1, D])
    lb_t = din("sgu_ln_b", [1, D])
    ws_t = din("sgu_ws", [1, 8, P, P])
    bs_t = din("sgu_bs", [1, 8, P])
    wpa_t = din("w_proj_a", [1, D, D])
    wpb_t = din("w_proj_b", [1, D, D])
    wo_t = din("w_out", [1, D, D])
    g2_t = din("norm_ffn_g", [1, D])
    wgu_t = din("w_gate_up", [1, D, 2 * DFF])
    wd_t = din("w_down", [1, DFF, D])
    gf_t = din("norm_final_g", [D])
    out_t = nc.dram_tensor("out", [NTOK, D], F32, kind="ExternalOutput")

    S_A = 0
    S_V = 8
    S_U = 10
    S_M = 12
    S_O = 20
    S_F = 22
    S_D = 33
    NS = 39
    scr_t = nc.dram_tensor("wscr", [NS, P, KC, 512], BF16, kind="Internal")

    sch = Sched()
    sems = {}

    A = {}

    def sb(name, shape, dt):
        A[name] = nc.alloc_sbuf_tensor(name, list(shape), dt)
        return A[name]

    X = [sb(f"X{b}", [P, NJ, D], F32) for b in range(2)]
    TMB = sb("TMB", [P, NJ, D], BF16)
    XT = [sb(f"XT{b}", [P, KC, T], BF16) for b in range(2)]
    RXT = [sb(f"RXT{q}", [P, 516], F32) for q in range(4)]
    HALO = sb("HALO", [P, KC, 4], F32)
    XCB = sb("XCB", [P, KC, T], BF16)
    ACC = [sb(f"ACC{q}", [P, T], F32) for q in range(2)]
    RA = [sb(f"RA{q}", [P, T], F32) for q in range(2)]
    A2 = [sb(f"A2{q}", [P, T], F32) for q in range(2)]
    IG = [sb(f"IG{q}", [P, T], F32) for q in range(2)]
    HH = [sb(f"HH{q}", [P, T], F32) for q in range(2)]
    GG = [sb(f"GG{q}", [P, T], BF16) for q in range(2)]
    HS = sb("HS", [P, KC], F32)
    YA = sb("YA", [P, KC, T], BF16)
    YB = sb("YB", [P, KC, T], BF16)
    MG = sb("MG", [P, KC, T], BF16)
    HM = sb("HM", [P, NF * T], BF16)
    HM32 = HM.bitcast(F32)
    GU = [sb(f"GU{q}", [P, T], BF16) for q in range(2)]
    T1 = [sb(f"T1{q}", [P, T], F32) for q in range(2)]
    SA = [sb(f"SA{q}", [P, T], F32) for q in range(2)]
    SB_ = [sb(f"SB{q}", [P, T], F32) for q in range(2)]
    SG = [sb(f"SG{q}", [P, T], BF16) for q in range(2)]
    JUNK = sb("JUNK", [P, D], BF16)
    RING = [sb(f"RING{r}", [P, KC, 512], BF16) for r in range(NRING)]
    IDF = sb("IDF", [P, P], F32)
    IDB = sb("IDB", [P, P], BF16)
    VS = sb("VS", [P, P], F32)
    CV = sb("CV", [P, 96], F32)
    CX = sb("CX", [P, 64], F32)
    WAB = sb("WAB", [P, KC, P], BF16)
    WXB = sb("WXB", [P, KC, P], BF16)
    WST = sb("WST", [P, 8, P], BF16)
    CBI = sb("CBI", [P, 8, P], F32)
    GFB = sb("GFB", [P, D], F32)
    FLG = sb("FLG", [P, 1], F32)
    ST = [sb(f"ST{q}", [P, 64], F32) for q in range(2)]

    PS = [nc.alloc_psum_tensor(f"ps{i}", [P, 512], F32) for i in range(8)]
    PSB = [p.bitcast(BF16) for p in PS]

    bX = [[Buf(f"X{b}_{j}") for j in range(NJ)] for b in range(2)]
    bTMB = [Buf(f"TMB{j}") for j in range(NJ)]
    bXT = [[Buf(f"XT{b}_{k}") for k in range(KC)] for b in range(2)]
    bRXT = [Buf(f"RXT{q}") for q in range(4)]
    bHALO = [Buf(f"HALO{c}", True) for c in range(KC)]
    bXCB = [Buf(f"XCB{c}") for c in range(KC)]
    bACC = [Buf(f"ACC{q}") for q in range(2)]
    bRA = [Buf(f"RA{q}") for q in range(2)]
    bA2 = [Buf(f"A2{q}") for q in range(2)]
    bIG = [Buf(f"IG{q}") for q in range(2)]
    bHH = [Buf(f"HH{q}") for q in range(2)]
    bGG = [Buf(f"GG{q}") for q in range(2)]
    bHS = [Buf(f"HS{c}", True) for c in range(KC)]
    bYA = [Buf(f"YA{c}") for c in range(KC)]
    bYB = [Buf(f"YB{c}") for c in range(KC)]
    bMG = [Buf(f"MG{c}") for c in range(KC)]
    bHM = [Buf(f"HM{f}") for f in range(NF)]
    bGU = [Buf(f"GU{q}") for q in range(2)]
    bT1 = [Buf(f"T1{q}") for q in range(2)]
    bSA = [Buf(f"SA{q}") for q in range(2)]
    bSB = [Buf(f"SB{q}") for q in range(2)]
    bSG = [Buf(f"SG{q}") for q in range(2)]
    bJUNK = Buf("JUNK")
    bRING = [Buf(f"RING{r}") for r in range(NRING)]
    bPS = [Buf(f"PS{i}") for i in range(8)]
    bCONST = Buf("CONST", True)
    bSCRA = Buf("SCRA")
    bSCRB = Buf("SCRB")
    bSETUP = Buf("SETUP", True)
    bST = {}

    def stbuf(par, name):
        key = (par, name)
        if key not in bST:
            bST[key] = Buf(f"ST{par}_{name}", True)
        return bST[key]

    def GVap(j, lo, hi):
        return HM32[:, j * D + lo: j * D + hi]

    def bGV(j):
        return [bHM[4 * j + i] for i in range(4)]

    psrot = [0]

    def ps_next():
        i = psrot[0]
        psrot[0] = (i + 1) % 8
        return i

    ring_state = {"n": 0, "pending_store": None, "since": 0}

    def wload(slot, width, scr_buf, nk=KC, col0=0):
        r = ring_state["n"] % NRING
        ring_state["n"] += 1
        src = scr_t.ap()[slot, :, 0:nk, col0:col0 + width]
        dst = RING[r][:, 0:nk, 0:width]
        sch.dma("sp", lambda e, dst=dst, src=src: e.dma_start(out=dst, in_=src), f"w{r}",
                reads=list(scr_buf), writes=[bRING[r]])
        ring_state["since"] += 1
        if ring_state["pending_store"] is not None and ring_state["since"] >= NRING:
            ring_state["pending_store"]()
            ring_state["pending_store"] = None
        if ring_state["pending_store"] is None and ring_state.get("pending_load") is not None:
            ring_state["pending_load"]()
            ring_state["pending_load"] = None
        return r

    def cst(v, k):
        return CV[:, v * 8 + k: v * 8 + k + 1]

    V_G1, V_G2, V_CW0, V_CB, V_BA, V_BX, V_LAM, V_LG = 0, 1, 2, 6, 7, 8, 9, 10
    X_C1, X_HC1, X_HBA, X_HBX = 0, 8, 16, 24

    def cx(base, k):
        return CX[:, base + k: base + k + 1]

    KSTAGE = int(os.environ.get('KSTAGE', '99'))

    def stage(n):
        sch.skip = n > KSTAGE

    stage(1)
    bZ = Buf("Z", True)

    def cdma(dst, src):
        sch.dma("sp", lambda e: e.dma_start(out=dst, in_=src), "cst", reads=[bZ], writes=[])
        bCONST.w = ("cst", sch.dmacount["cst"])

    sch.op("dve", lambda e: e.memset(VS[:], 0.0), writes=[bZ])
    sch.op("dve", lambda e: e.memset(HM32[:, 2048:4096], 0.0), writes=[bZ])
    vecs = [g1_t.ap()[0], g2_t.ap()[0], cw_t.ap()[0, 0], cw_t.ap()[0, 1], cw_t.ap()[0, 2], cw_t.ap()[0, 3],
            cb_t.ap()[0], ba_t.ap()[0].rearrange("h d -> (h d)"), bx_t.ap()[0].rearrange("h d -> (h d)"),
            lam_t.ap()[0], lg_t.ap()[0]]
    for v, ap in enumerate(vecs):
        cdma(VS[v * 8:(v + 1) * 8, :], ap.rearrange("(k p) -> k p", p=P))
    cdma(FLG[:], flag_t.ap())
    cdma(GFB[:], bass.AP(gf_t, 0, [[0, P], [1, D]]))
    WS_F = HM32[:, 0:1024].rearrange("p (g s) -> p g s", g=8)
    WST_F = HM32[:, 1024:2048].rearrange("p (g s) -> p g s", g=8)
    WA_F = HM32[:, 2048:3072].rearrange("p (g s) -> p g s", g=8)
    WX_F = HM32[:, 3072:4096].rearrange("p (g s) -> p g s", g=8)
    LB_F = HM32[:, 4096:5120]
    bSTG = Buf("STG", True)
    sdma = cdma

    sdma(WS_F, ws_t.ap()[0].rearrange("g t s -> t g s"))
    for q in range(2):
        sdma(WA_F[64 * q:64 * q + 64, :, 64 * q:64 * q + 64],
             wa_t.ap()[0].rearrange("(c q) d e -> q d c e", q=2)[q])
        sdma(WX_F[64 * q:64 * q + 64, :, 64 * q:64 * q + 64],
             wx_t.ap()[0].rearrange("(c q) d e -> q d c e", q=2)[q])
    sdma(LB_F, bass.AP(lb_t, 0, [[0, P], [1, D]]))
    BSB = X[1][:, 0, :]
    sch.dma("sp", lambda e: e.dma_start(out=BSB, in_=bass.AP(bs_t, 0, [[0, P], [1, D]])), "cst",
            reads=[], writes=[bX[1][0]])
    bCONST.w = ("cst", sch.dmacount["cst"])

    stage(2)
    sch.op("pool", lambda e: e.memset(IDF[:], 0.0), writes=[bSETUP])
    sch.op("pool", lambda e: e.affine_select(out=IDF[:], in_=IDF[:], pattern=[[-1, P]], base=0,
                                             channel_multiplier=1, compare_op=ALU.not_equal, fill=1.0),
           reads=[bSETUP], writes=[bSETUP])
    sch.op("act", lambda e: e.activation(out=IDB[:], in_=IDF[:], func=AF.Copy), reads=[bSETUP], writes=[bSETUP])
    stage(3)
    sch.op("pool", lambda e: e.affine_select(out=WS_F, in_=WS_F, pattern=[[0, 8], [-1, P]], base=0,
                                             channel_multiplier=1, compare_op=ALU.is_ge, fill=0.0),
           reads=[bCONST, bSTG], writes=[bSTG])
    stage(4)
    pi = ps_next()
    sch.op("pe", lambda e, pi=pi: e.transpose(out=PS[pi][:, 0:88], in_=VS[0:88, :], identity=IDF[0:88, 0:88]),
           reads=[bCONST, bSETUP], writes=[bPS[pi]])
    sch.op("dve", lambda e, pi=pi: e.tensor_copy(out=CV[:, 0:88], in_=PS[pi][:, 0:88]),
           reads=[bPS[pi]], writes=[bSETUP])
    stage(5)
    bCXt = Buf("CXt", True)
    sch.op("act", lambda e: e.activation(out=CX[:, 32:40], in_=CV[:, V_LAM * 8:V_LAM * 8 + 8], func=AF.Exp, scale=-1.0),
           reads=[bSETUP], writes=[bCXt])
    sch.op("act", lambda e: e.activation(out=CX[:, 40:48], in_=CX[:, 32:40], func=AF.Ln, bias=1.0),
           reads=[bCXt], writes=[bCXt])
    sch.op("dve", lambda e: e.tensor_scalar(out=CX[:, X_C1:X_C1 + 8], in0=CX[:, 40:48], scalar1=-8.0, scalar2=None,
                                            op0=ALU.mult), reads=[bCXt], writes=[bSETUP])
    sch.op("dve", lambda e: e.tensor_scalar(out=CX[:, X_HC1:X_HC1 + 8], in0=CX[:, 40:48], scalar1=-4.0, scalar2=None,
                                            op0=ALU.mult), reads=[bCXt], writes=[bSETUP])
    sch.op("dve", lambda e: e.tensor_scalar(out=CX[:, X_HBA:X_HBA + 8], in0=CV[:, V_BA * 8:V_BA * 8 + 8], scalar1=0.5,
                                            scalar2=None, op0=ALU.mult), reads=[bSETUP], writes=[bSETUP])
    sch.op("dve", lambda e: e.tensor_scalar(out=CX[:, X_HBX:X_HBX + 8], in0=CV[:, V_BX * 8:V_BX * 8 + 8], scalar1=0.5,
                                            scalar2=None, op0=ALU.mult), reads=[bSETUP], writes=[bSETUP])
    stage(6)
    sch.op("act", lambda e: e.activation(out=WAB[:], in_=WA_F, func=AF.Copy), reads=[bSTG, bCONST], writes=[bSETUP])
    sch.op("act", lambda e: e.activation(out=WXB[:], in_=WX_F, func=AF.Copy), reads=[bSTG, bCONST], writes=[bSETUP])
    stage(7)
    for g in range(8):
        pi = ps_next()
        sch.op("pe", lambda e, pi=pi, g=g: e.transpose(out=PS[pi][:, 0:P], in_=WS_F[:, g, :], identity=IDF[:]),
               reads=[bSTG, bSETUP], writes=[bPS[pi]])
        if 's7a' in DBG:
            continue
        sch.op("dve", lambda e, pi=pi, g=g: e.tensor_copy(out=WST_F[:, g, :], in_=PS[pi][:, 0:P]),
               reads=[bPS[pi]], writes=[bSTG])
    if 's7b' not in DBG:
        sch.op("act", lambda e: e.activation(out=WST[:], in_=WST_F, func=AF.Copy), reads=[bSTG], writes=[bSETUP])
    stage(8)
    for g in range(8):
        pi = ps_next()

        def mm(e, pi=pi, g=g):
            return e.matmul(PS[pi][:, 0:P], lhsT=LB_F[:, g * P:(g + 1) * P], rhs=WST_F[:, g, :], start=True, stop=True)

        sch.op("pe", mm, reads=[bSTG, bSETUP, bCONST], writes=[bPS[pi]])
        sch.op("dve", lambda e, pi=pi, g=g: e.tensor_tensor(out=CBI[:, g, :], in0=PS[pi][:, 0:P],
                                                            in1=BSB[:, g * P:(g + 1) * P], op=ALU.add),
               reads=[bPS[pi], bCONST, bX[1][0]], writes=[bSETUP])
    stage(0)
    sch.op("dve", lambda e: e.memset(HS[:], 0.0), writes=bHS)
    sch.op("dve", lambda e: e.memset(HALO[:], 0.0), writes=bHALO)

    cvk = [0]
    NO_CONV = 'noconv' in DBG
    bCVs = [Buf(f"cv{i}") for i in range(4)]

    def cast_dma(slot, col0, src_ap_rows_cols):
        kc = src_ap_rows_cols.shape[0] // P
        w = src_ap_rows_cols.shape[1]
        src = src_ap_rows_cols.rearrange("(kc p) n -> p kc n", p=P)
        dst = scr_t.ap()[slot, :, 0:kc, col0:col0 + w]
        k = cvk[0]
        cvk[0] += 1
        if NO_CONV:
            return
        sch.dma("pool", lambda e: e.dma_start(out=dst, in_=src), f"cv{k % 4}", reads=[], writes=[bCVs[k % 4]])

    win = win_t.ap()[0]
    wgu = wgu_t.ap()[0]
    wdn = wd_t.ap()[0]
    for c in range(8):
        cast_dma(S_A + c, 0, win[:, c * P:(c + 1) * P])
    bSCRA_l = []
    for i in range(4):
        fb = Buf(f"scra{i}")
        fb.w = bCVs[i].w
        bSCRA_l.append(fb)
    for c in range(8):
        cast_dma(S_A + c, P, win[:, D + c * P: D + (c + 1) * P])
    for h in range(2):
        cast_dma(S_V + h, 0, win[:, 3 * D + h * 512: 3 * D + (h + 1) * 512])
    for h in range(2):
        cast_dma(S_U + h, 0, win[:, 2 * D + h * 512: 2 * D + (h + 1) * 512])
    for m in range(8):
        cast_dma(S_M + m, 0, wpa_t.ap()[0][:, m * P:(m + 1) * P])
        cast_dma(S_M + m, P, wpb_t.ap()[0][:, m * P:(m + 1) * P])
        cast_dma(S_M + m, 2 * P, win[:, 4 * D + m * P: 4 * D + (m + 1) * P])
        cast_dma(S_M + m, 3 * P, win[:, 5 * D + m * P: 5 * D + (m + 1) * P])
    for h in range(2):
        cast_dma(S_O + h, 0, wo_t.ap()[0][:, h * 512:(h + 1) * 512])
    for s in range(11):
        cast_dma(S_F + s, 0, wgu[:, s * 256:(s + 1) * 256])
        cast_dma(S_F + s, 256, wgu[:, DFF + s * 256: DFF + (s + 1) * 256])
    for h in range(2):
        for q in range(3):
            f0 = 8 * q
            f1 = min(8 * q + 8, NF)
            cast_dma(S_D + 3 * h + q, 0, wdn[f0 * P: f1 * P, h * 512:(h + 1) * 512])
    bSCRB_l = []
    for i in range(4):
        fb = Buf(f"scrb{i}")
        fb.w = bCVs[i].w
        bSCRB_l.append(fb)


    tile_ctr = [0]

    def load_x(b, src_t, t0, eng="sp"):
        src = src_t.ap()[t0:t0 + T, :].rearrange("(j p) d -> p j d", p=P)
        sem = f"x{b}" if eng == "sp" else f"xs{b}"
        sch.dma(eng, lambda e: e.dma_start(out=X[b][:], in_=src), sem, reads=[], writes=bX[b])

    def rstd_from_ssq(par, name_ssq, name_out, col_ssq, col_out, n):
        st = ST[par]
        sch.op("dve", lambda e: e.tensor_scalar(out=st[:, col_out:col_out + n], in0=st[:, col_ssq:col_ssq + n],
                                                scalar1=1.0 / D, scalar2=EPS, op0=ALU.mult, op1=ALU.add),
               reads=[stbuf(par, name_ssq)], writes=[stbuf(par, name_out)])
        sch.op("act", lambda e: e.activation(out=st[:, col_out:col_out + n], in_=st[:, col_out:col_out + n], func=AF.Sqrt),
               reads=[stbuf(par, name_out)], writes=[stbuf(par, name_out)])
        sch.op("dve", lambda e: e.reciprocal(out=st[:, col_out:col_out + n], in_=st[:, col_out:col_out + n]),
               reads=[stbuf(par, name_out)], writes=[stbuf(par, name_out)])

    def norm_part(b, par, tag, c0):
        st = ST[par]
        for j in range(NJ):
            sch.op("act", lambda e, j=j: e.activation(out=JUNK[:], in_=X[b][:, j, :], func=AF.Square,
                                                     accum_out=st[:, c0 + j:c0 + j + 1]),
                   reads=[bX[b][j]], writes=[stbuf(par, tag + "ssq")])
        rstd_from_ssq(par, tag + "ssq", tag + "rstd", c0, c0 + 4, 4)
        for j in range(NJ):
            sch.op("act", lambda e, j=j: e.activation(out=TMB[:, j, :], in_=X[b][:, j, :], func=AF.Copy,
                                                     scale=st[:, c0 + 4 + j:c0 + 5 + j]),
                   reads=[bX[b][j], stbuf(par, tag + "rstd")], writes=[bTMB[j]])

    def transpose_part(xtb, gvec):
        for k in range(KC):
            pi = ps_next()

            def tr(e, pi=pi, k=k):
                last = None
                for j in range(NJ):
                    last = e.transpose(out=PSB[pi][:, j * P:(j + 1) * P], in_=TMB[:, j, k * P:(k + 1) * P],
                                       identity=IDB[:])
                return last

            sch.op("pe", tr, reads=bTMB + [bSETUP], writes=[bPS[pi]])
            sch.op("act", lambda e, pi=pi, k=k: e.activation(out=XT[xtb][:, k, :], in_=PSB[pi][:, 0:T], func=AF.Copy,
                                                            scale=cst(gvec, k)),
                   reads=[bPS[pi], bSETUP], writes=[bXT[xtb][k]])

    def norm_and_transpose(b, par, xtb, gvec, tag, c0):
        norm_part(b, par, tag, c0)
        transpose_part(xtb, gvec)

    def fm_matmul(pi, r, col0, rhs_t, rhs_bufs):
        def mm(e):
            last = None
            for k in range(KC):
                last = e.matmul(PS[pi][:, :], lhsT=RING[r][:, k, col0:col0 + P], rhs=rhs_t[:, k, :],
                                start=(k == 0), stop=(k == KC - 1))
            return last
        sch.op("pe", mm, reads=[bRING[r], bSETUP] + rhs_bufs, writes=[bPS[pi]])

    RA4 = [RA[0], RA[1], SA[0], SA[1]]
    bRA4 = [bRA[0], bRA[1], bSA[0], bSA[1]]
    A24 = [A2[0], A2[1], SB_[0], SB_[1]]
    bA24 = [bA2[0], bA2[1], bSB[0], bSB[1]]
    IG4 = [IG[0], IG[1], T1[0], T1[1]]
    bIG4 = [bIG[0], bIG[1], bT1[0], bT1[1]]

    def branch_a(xtb, main, scr_buf, mid_hook=None, pre_hook=None):
        tt_eng = "pool" if (main and USE_POOL) else "dve"
        rr = {}

        def ph1(cs):
            for c in cs:
                q = c % 4
                qa = c % 2
                r = wload(S_A + c, P, scr_buf)
                p1 = ps_next()
                fm_matmul(p1, r, 0, XT[xtb], bXT[xtb])
                sch.op("act", lambda e, c=c, q=q: e.activation(out=RXT[q][:, 0:3], in_=HALO[:, c, 0:3], func=AF.Copy),
                       reads=[bHALO[c]], writes=[bRXT[q]])
                sch.op("act", lambda e, q=q, p1=p1: e.activation(out=RXT[q][:, 3:515], in_=PS[p1][:, :], func=AF.Copy),
                       reads=[bPS[p1]], writes=[bRXT[q]])
                sch.op("act", lambda e, c=c, q=q: e.activation(out=HALO[:, c, 0:3], in_=RXT[q][:, 512:515], func=AF.Copy),
                       reads=[bRXT[q]], writes=[bHALO[c]])
                sch.op("dve", lambda e, c=c, q=q, qa=qa: e.tensor_scalar(out=ACC[qa][:], in0=RXT[q][:, 0:512],
                                                                         scalar1=cst(V_CW0 + 0, c), scalar2=cst(V_CB, c),
                                                                         op0=ALU.mult, op1=ALU.add),
                       reads=[bRXT[q], bSETUP], writes=[bACC[qa]])
                for kk in (1, 2):
                    sch.op("dve", lambda e, c=c, q=q, qa=qa, kk=kk: e.scalar_tensor_tensor(
                        out=ACC[qa][:], in0=RXT[q][:, kk:kk + 512], scalar=cst(V_CW0 + kk, c), in1=ACC[qa][:],
                        op0=ALU.mult, op1=ALU.add), reads=[bRXT[q], bACC[qa], bSETUP], writes=[bACC[qa]])
                sch.op("dve", lambda e, c=c, q=q, qa=qa: e.scalar_tensor_tensor(
                    out=XCB[:, c, :], in0=RXT[q][:, 3:515], scalar=cst(V_CW0 + 3, c), in1=ACC[qa][:],
                    op0=ALU.mult, op1=ALU.add), reads=[bRXT[q], bACC[qa], bSETUP], writes=[bXCB[c]])

        def ph24(cs):
            for c in cs:
                q = c % 4
                p2 = ps_next()
                p3 = ps_next()
                sch.op("pe", lambda e, c=c, p2=p2: e.matmul(PS[p2][:, :], lhsT=WAB[:, c, :], rhs=XCB[:, c, :],
                                                            start=True, stop=True),
                       reads=[bXCB[c], bSETUP], writes=[bPS[p2]])
                sch.op("pe", lambda e, c=c, p3=p3: e.matmul(PS[p3][:, :], lhsT=WXB[:, c, :], rhs=XCB[:, c, :],
                                                            start=True, stop=True),
                       reads=[bXCB[c], bSETUP], writes=[bPS[p3]])
                sch.op("act", lambda e, c=c, q=q, p2=p2: e.activation(out=RA4[q][:], in_=PS[p2][:, :], func=AF.Tanh,
                                                                     scale=0.5, bias=cx(X_HBA, c)),
                       reads=[bPS[p2], bSETUP], writes=[bRA4[q]])
                sch.op("act", lambda e, c=c, q=q, p3=p3: e.activation(out=IG4[q][:], in_=PS[p3][:, :], func=AF.Tanh,
                                                                     scale=0.5, bias=cx(X_HBX, c)),
                       reads=[bPS[p3], bSETUP], writes=[bIG4[q]])
            for c in cs:
                q = c % 4
                sch.op("act", lambda e, c=c, q=q: e.activation(out=A24[q][:], in_=RA4[q][:], func=AF.Exp,
                                                              scale=cx(X_C1, c), bias=cx(X_C1, c)),
                       reads=[bRA4[q], bSETUP], writes=[bA24[q]])
                sch.op("act", lambda e, c=c, q=q: e.activation(out=RA4[q][:], in_=RA4[q][:], func=AF.Exp,
                                                              scale=cx(X_HC1, c), bias=cx(X_HC1, c)),
                       reads=[bRA4[q], bSETUP], writes=[bRA4[q]])
                sch.op("dve", lambda e, c=c, q=q: e.scalar_tensor_tensor(out=IG4[q][:], in0=IG4[q][:], scalar=1.0,
                                                                         in1=XCB[:, c, :], op0=ALU.add, op1=ALU.mult),
                       reads=[bIG4[q], bXCB[c]], writes=[bIG4[q]])
            for c in cs:
                q = c % 4
                sch.op("act", lambda e, q=q: e.activation(out=A24[q][:], in_=A24[q][:], func=AF.Sqrt, scale=-1.0, bias=1.0),
                       reads=[bA24[q]], writes=[bA24[q]])

        def ph5(cs):
            for c in cs:
                q = c % 4
                hq = c % 2
                sch.op("dve", lambda e, q=q: e.scalar_tensor_tensor(out=IG4[q][:], in0=IG4[q][:], scalar=0.5,
                                                                    in1=A24[q][:], op0=ALU.mult, op1=ALU.mult),
                       reads=[bIG4[q], bA24[q]], writes=[bIG4[q]])
                sch.op("dve", lambda e, c=c, q=q, hq=hq: e.tensor_tensor_scan(
                    out=HH[hq][:], data0=RA4[q][:], data1=IG4[q][:], initial=HS[:, c:c + 1], op0=ALU.mult, op1=ALU.add),
                    reads=[bRA4[q], bIG4[q], bHS[c]], writes=[bHH[hq]])
                sch.op("dve", lambda e, c=c, hq=hq: e.tensor_copy(out=HS[:, c:c + 1], in_=HH[hq][:, T - 1:T]),
                       reads=[bHH[hq]], writes=[bHS[c]])
                if main:
                    p4 = ps_next()
                    r4 = wload(S_A + c, P, scr_buf, col0=P)
                    fm_matmul(p4, r4, 0, XT[xtb], bXT[xtb])
                    sch.op("act", lambda e, hq=hq, p4=p4: e.activation(out=GG[hq][:], in_=PS[p4][:, :],
                                                                      func=AF.Gelu_apprx_tanh),
                           reads=[bPS[p4]], writes=[bGG[hq]])
                    sch.op(tt_eng, lambda e, c=c, hq=hq: e.tensor_tensor(out=YA[:, c, :], in0=GG[hq][:], in1=HH[hq][:],
                                                                         op=ALU.mult),
                           reads=[bGG[hq], bHH[hq]], writes=[bYA[c]])

        g0, g1 = [0, 1, 2, 3], [4, 5, 6, 7]
        ph1(g0)
        ph1(g1)
        if pre_hook is not None:
            pre_hook()
        ph24(g0)
        if mid_hook is not None:
            mid_hook()
        ph5(g0)
        ph24(g1)
        ph5(g1)

    def branch_b_v(xtb, par, part=None):
        st = ST[par]
        C_VS, C_VQ, C_MS, C_MEAN, C_M2, C_VAR, C_RSTD, C_NB = 16, 24, 28, 32, 36, 40, 44, 48
        for h in ((0, 1) if part is None else (part,)):
            r = wload(S_V + h, 512, bSCRB_l)
            for j in range(NJ):
                pi = ps_next()

                def mm(e, pi=pi, r=r, j=j):
                    last = None
                    for k in range(KC):
                        last = e.matmul(PS[pi][:, :], lhsT=XT[xtb][:, k, j * P:(j + 1) * P], rhs=RING[r][:, k, :],
                                        start=(k == 0), stop=(k == KC - 1))
                    return last
                sch.op("pe", mm, reads=[bRING[r]] + bXT[xtb], writes=[bPS[pi]])
                sch.op("act", lambda e, pi=pi, j=j, h=h: e.activation(
                    out=GVap(j, h * 512, (h + 1) * 512), in_=PS[pi][:, :], func=AF.Gelu_apprx_tanh,
                    accum_out=st[:, C_VS + 2 * j + h:C_VS + 2 * j + h + 1]),
                    reads=[bPS[pi]], writes=[bGV(j)[2 * h], bGV(j)[2 * h + 1], stbuf(par, "vs")])
        if part == 0:
            return
        for j in range(NJ):
            sch.op("act", lambda e, j=j: e.activation(out=JUNK[:], in_=GVap(j, 0, D), func=AF.Square,
                                                     accum_out=st[:, C_VQ + j:C_VQ + j + 1]),
                   reads=bGV(j), writes=[stbuf(par, "vq")])
        vs3 = st[:, C_VS:C_VS + 8].rearrange("p (j h) -> p j h", h=2)
        sch.op("dve", lambda e: e.tensor_tensor(out=st[:, C_MS:C_MS + 4], in0=vs3[:, :, 0], in1=vs3[:, :, 1], op=ALU.add),
               reads=[stbuf(par, "vs")], writes=[stbuf(par, "ms")])
        sch.op("dve", lambda e: e.tensor_scalar(out=st[:, C_MEAN:C_MEAN + 4], in0=st[:, C_MS:C_MS + 4], scalar1=1.0 / D,
                                                scalar2=None, op0=ALU.mult),
               reads=[stbuf(par, "ms")], writes=[stbuf(par, "mean")])
        sch.op("dve", lambda e: e.tensor_tensor(out=st[:, C_M2:C_M2 + 4], in0=st[:, C_MEAN:C_MEAN + 4],
                                                in1=st[:, C_MEAN:C_MEAN + 4], op=ALU.mult),
               reads=[stbuf(par, "mean")], writes=[stbuf(par, "m2")])
        sch.op("dve", lambda e: e.scalar_tensor_tensor(out=st[:, C_VAR:C_VAR + 4], in0=st[:, C_VQ:C_VQ + 4], scalar=1.0 / D,
                                                       in1=st[:, C_M2:C_M2 + 4], op0=ALU.mult, op1=ALU.subtract),
               reads=[stbuf(par, "vq"), stbuf(par, "m2")], writes=[stbuf(par, "var")])
        sch.op("dve", lambda e: e.tensor_scalar(out=st[:, C_RSTD:C_RSTD + 4], in0=st[:, C_VAR:C_VAR + 4], scalar1=EPS,
                                                scalar2=None, op0=ALU.add),
               reads=[stbuf(par, "var")], writes=[stbuf(par, "vrstd")])
        sch.op("act", lambda e: e.activation(out=st[:, C_RSTD:C_RSTD + 4], in_=st[:, C_RSTD:C_RSTD + 4], func=AF.Sqrt),
               reads=[stbuf(par, "vrstd")], writes=[stbuf(par, "vrstd")])
        sch.op("dve", lambda e: e.reciprocal(out=st[:, C_RSTD:C_RSTD + 4], in_=st[:, C_RSTD:C_RSTD + 4]),
               reads=[stbuf(par, "vrstd")], writes=[stbuf(par, "vrstd")])
        sch.op("dve", lambda e: e.scalar_tensor_tensor(out=st[:, C_NB:C_NB + 4], in0=st[:, C_MEAN:C_MEAN + 4], scalar=-1.0,
                                                       in1=st[:, C_RSTD:C_RSTD + 4], op0=ALU.mult, op1=ALU.mult),
               reads=[stbuf(par, "mean"), stbuf(par, "vrstd")], writes=[stbuf(par, "nb")])
        for j in range(NJ):
            sch.op("act", lambda e, j=j: e.activation(out=TMB[:, j, :], in_=GVap(j, 0, D), func=AF.Identity,
                                                     scale=st[:, C_RSTD + j:C_RSTD + j + 1],
                                                     bias=st[:, C_NB + j:C_NB + j + 1]),
                   reads=bGV(j) + [stbuf(par, "vrstd"), stbuf(par, "nb")], writes=[bTMB[j]])

    def branch_b_mix(xtb):
        ru = None
        for g in range(8):
            q = g % 2
            if g % 4 == 0:
                ru = wload(S_U + g // 4, 512, bSCRB_l)
            pm = ps_next()

            def mix(e, pm=pm, g=g):
                last = None
                for j in range(NJ):
                    last = e.matmul(PS[pm][:, j * P:(j + 1) * P], lhsT=TMB[:, j, g * P:(g + 1) * P], rhs=WST[:, g, :],
                                    start=True, stop=True)
                return last
            sch.op("pe", mix, reads=bTMB + [bSETUP], writes=[bPS[pm]])
            pu = ps_next()
            fm_matmul(pu, ru, (g % 4) * P, XT[xtb], bXT[xtb])
            sch.op("act", lambda e, q=q, pu=pu: e.activation(out=GU[q][:], in_=PS[pu][:, :], func=AF.Gelu_apprx_tanh),
                   reads=[bPS[pu]], writes=[bGU[q]])
            cb_ap = bass.AP(CBI, g * P, [[8 * P, P], [0, NJ], [1, P]])
            sch.op("dve", lambda e, q=q, pm=pm, g=g, cb_ap=cb_ap: e.scalar_tensor_tensor(
                out=T1[q][:].rearrange("p (j t) -> p j t", j=NJ), in0=PS[pm][:, :].rearrange("p (j t) -> p j t", j=NJ),
                scalar=cst(V_LG, g), in1=cb_ap, op0=ALU.mult, op1=ALU.add),
                reads=[bPS[pm], bSETUP], writes=[bT1[q]])
            sch.op("pool" if USE_POOL else "dve",
                   lambda e, q=q, g=g: e.tensor_tensor(out=YB[:, g, :], in0=T1[q][:], in1=GU[q][:], op=ALU.mult),
                   reads=[bT1[q], bGU[q]], writes=[bYB[g]])

    def merge(xtb):
        for m in range(8):
            q = m % 2
            r = wload(S_M + m, 512, bSCRB_l)
            pa, pb, pga, pgb = ps_next(), ps_next(), ps_next(), ps_next()
            fm_matmul(pga, r, 2 * P, XT[xtb], bXT[xtb])
            fm_matmul(pgb, r, 3 * P, XT[xtb], bXT[xtb])
            fm_matmul(pa, r, 0, YA, bYA)
            fm_matmul(pb, r, P, YB, bYB)
            sch.op("act", lambda e, q=q, pga=pga: e.activation(out=SA[q][:], in_=PS[pga][:, :], func=AF.Tanh, scale=0.5),
                   reads=[bPS[pga]], writes=[bSA[q]])
            sch.op("act", lambda e, q=q, pgb=pgb: e.activation(out=SB_[q][:], in_=PS[pgb][:, :], func=AF.Tanh, scale=0.5),
                   reads=[bPS[pgb]], writes=[bSB[q]])
            sch.op("dve", lambda e, q=q, pa=pa: e.scalar_tensor_tensor(out=SA[q][:], in0=SA[q][:], scalar=1.0,
                                                                       in1=PS[pa][:, :], op0=ALU.add, op1=ALU.mult),
                   reads=[bSA[q], bPS[pa]], writes=[bSA[q]])
            sch.op("dve", lambda e, q=q, pb=pb: e.scalar_tensor_tensor(out=SB_[q][:], in0=SB_[q][:], scalar=1.0,
                                                                       in1=PS[pb][:, :], op0=ALU.add, op1=ALU.mult),
                   reads=[bSB[q], bPS[pb]], writes=[bSB[q]])
            sch.op("pool" if USE_POOL else "dve",
                   lambda e, q=q, m=m: e.tensor_tensor(out=MG[:, m, :], in0=SA[q][:], in1=SB_[q][:], op=ALU.add),
                   reads=[bSA[q], bSB[q]], writes=[bMG[m]])

    def out_proj(b):
        for h in range(2):
            r = wload(S_O + h, 512, bSCRB_l)
            for j in range(NJ):
                pi = ps_next()

                def mm(e, pi=pi, r=r, j=j):
                    last = None
                    for k in range(KC):
                        last = e.matmul(PS[pi][:, :], lhsT=MG[:, k, j * P:(j + 1) * P], rhs=RING[r][:, k, :],
                                        start=(k == 0), stop=(k == KC - 1))
                    return last
                sch.op("pe", mm, reads=[bRING[r]] + bMG, writes=[bPS[pi]])
                sch.op("dve", lambda e, pi=pi, j=j, h=h: e.scalar_tensor_tensor(
                    out=X[b][:, j, h * 512:(h + 1) * 512], in0=PS[pi][:, :], scalar=0.5,
                    in1=X[b][:, j, h * 512:(h + 1) * 512], op0=ALU.mult, op1=ALU.add),
                    reads=[bPS[pi], bX[b][j]], writes=[bX[b][j]])

    def ffn(b, xtb, hook_a=None, hook_b=None):
        for s in range(11):
            r = wload(S_F + s, 512, bSCRB_l)
            for e2 in range(2):
                f = 2 * s + e2
                q = f % 2
                pg, pu = ps_next(), ps_next()
                fm_matmul(pg, r, e2 * P, XT[xtb], bXT[xtb])
                fm_matmul(pu, r, 256 + e2 * P, XT[xtb], bXT[xtb])
                sch.op("act", lambda e, q=q, pg=pg: e.activation(out=SG[q][:], in_=PS[pg][:, :], func=AF.Silu),
                       reads=[bPS[pg]], writes=[bSG[q]])
                sch.op("dve", lambda e, q=q, pu=pu, f=f: e.tensor_tensor(out=HM[:, f * T:(f + 1) * T], in0=PS[pu][:, :], in1=SG[q][:],
                                                                        op=ALU.mult),
                       reads=[bPS[pu], bSG[q]], writes=[bHM[f]])
        if hook_a is not None:
            hook_a()
        for h in range(2):
            pj = [ps_next() for _ in range(NJ)]
            for qq in range(3):
                nfl = min(8, NF - 8 * qq)
                r = wload(S_D + 3 * h + qq, 512, bSCRB_l, nfl)

                def mm(e, r=r, qq=qq, nfl=nfl, pj=pj):
                    last = None
                    for fl in range(nfl):
                        f = 8 * qq + fl
                        for j in range(NJ):
                            last = e.matmul(PS[pj[j]][:, :], lhsT=HM[:, f * T + j * P:f * T + (j + 1) * P], rhs=RING[r][:, fl, :],
                                            start=(f == 0), stop=(f == NF - 1))
                    return last
                sch.op("pe", mm, reads=[bRING[r]] + bHM[8 * qq:8 * qq + nfl], writes=[bPS[i] for i in pj])
            for j in range(NJ):
                sch.op("dve", lambda e, j=j, h=h, pj=pj: e.tensor_tensor(
                    out=X[b][:, j, h * 512:(h + 1) * 512], in0=PS[pj[j]][:, :], in1=X[b][:, j, h * 512:(h + 1) * 512],
                    op=ALU.add), reads=[bPS[pj[j]], bX[b][j]], writes=[bX[b][j]])

        if hook_b is not None:
            hook_b()

    def final_norm_store(b, par, t0):
        st = ST[par]
        c0 = 56
        for j in range(NJ):
            sch.op("act", lambda e, j=j: e.activation(out=JUNK[:], in_=X[b][:, j, :], func=AF.Square,
                                                     accum_out=st[:, c0 + j:c0 + j + 1]),
                   reads=[bX[b][j]], writes=[stbuf(par, "fssq")])
        rstd_from_ssq(par, "fssq", "frstd", c0, c0 + 4, 4)
        for j in range(NJ):
            sch.op("dve", lambda e, j=j: e.scalar_tensor_tensor(out=X[b][:, j, :], in0=X[b][:, j, :],
                                                                scalar=st[:, c0 + 4 + j:c0 + 5 + j], in1=GFB[:],
                                                                op0=ALU.mult, op1=ALU.mult),
                   reads=[bX[b][j], stbuf(par, "frstd"), bCONST], writes=[bX[b][j]])
        dst = out_t.ap()[t0:t0 + T, :].rearrange("(j p) d -> p j d", p=P)

        if IO_ON_POOL:
            sch.dma("pool", lambda e: e.dma_start(out=dst, in_=X[b][:]), f"o{b}", reads=bX[b], writes=[])
        else:
            def do_store():
                sch.dma("sp", lambda e: e.dma_start(out=dst, in_=X[b][:]), f"o{b}", reads=bX[b], writes=[])
            ring_state["pending_store"] = do_store
            ring_state["since"] = 0

    seq = [("pre", i) for i in range(n_pre)] + [("main", i) for i in range(n_main)]
    if 'setuponly' in DBG:
        seq = []
        load_x(0, xm_t, 0)
        dst0 = out_t.ap()[0:T, :].rearrange("(j p) d -> p j d", p=P)
        sch.op("dve", lambda e: e.tensor_copy(out=X[1][:, 0, 0:P], in_=CBI[:, 7, :]), reads=[bSETUP], writes=bX[1])
        sch.dma("sp", lambda e: e.dma_start(out=dst0, in_=X[0][:]), "o0", reads=bX[0] + bX[1], writes=[])
    if seq:
        kind, i = seq[0]
        load_x(0, xp_t if kind == "pre" else xm_t, i * T)
    if seq:
        norm_part(0, 0, "n1", 0)
        transpose_part(0, V_G1)
    for n, (kind, i) in enumerate(seq):
        b = n % 2
        par = n % 2
        has_next = n + 1 < len(seq)
        nb, npar = (n + 1) % 2, (n + 1) % 2
        if has_next:
            k2, i2 = seq[n + 1]

            def do_load(n=n, k2=k2, i2=i2, kind=kind):
                load_x((n + 1) % 2, xp_t if k2 == "pre" else xm_t, i2 * T,
                       eng="pool" if (IO_ON_POOL and kind == "main") else "sp")
            if ring_state["pending_store"] is None:
                do_load()
            else:
                ring_state["pending_load"] = do_load

        def next_norm(nb=nb, npar=npar):
            if ring_state.get("pending_load") is not None:
                if ring_state["pending_store"] is not None:
                    ring_state["pending_store"]()
                    ring_state["pending_store"] = None
                ring_state["pending_load"]()
                ring_state["pending_load"] = None
            norm_part(nb, npar, "n1", 0)

        def next_transpose():
            transpose_part(0, V_G1)

        if kind == "pre":
            branch_a(0, False, bSCRA_l)
            if i == n_pre - 1:
                sch.op("dve", lambda e: e.tensor_scalar(out=HS[:], in0=HS[:], scalar1=FLG[:, 0:1], scalar2=None,
                                                        op0=ALU.mult), reads=bHS + [bCONST], writes=bHS)
            if has_next:
                next_norm()
                next_transpose()
        else:
            if HOIST_V:
                branch_a(0, True, bSCRB_l, pre_hook=lambda par=par: branch_b_v(0, par, 0),
                         mid_hook=lambda par=par: branch_b_v(0, par, 1))
            else:
                branch_a(0, True, bSCRB_l)
                branch_b_v(0, par)
            branch_b_mix(0)
            merge(0)
            out_proj(b)
            norm_and_transpose(b, par, 1, V_G2, "n2", 8)
            if PIPE_NEXT:
                ffn(b, 1, hook_a=next_norm if has_next else None, hook_b=next_transpose if has_next else None)
                final_norm_store(b, par, i * T)
            else:
                ffn(b, 1)
                final_norm_store(b, par, i * T)
                if has_next:
                    next_norm()
                    next_transpose()
    if ring_state["pending_store"] is not None:
        ring_state["pending_store"]()
        ring_state["pending_store"] = None
    final_waits = [(k, v) for k, v in sch.dmacount.items() if k.startswith("o")]

    semnames = ["pe", "act", "dve", "pool"] + sorted(sch.dmacount.keys())
    import contextlib
    with contextlib.ExitStack() as es:
        for s in semnames:
            sems[s] = es.enter_context(nc.semaphore(f"s_{s}"))
        block = es.enter_context(nc.Block())

        def run(engobj, name):
            for waits, emit, (semname, inc) in sch.streams[name]:
                for k, v in waits:
                    engobj.wait_ge(sems[k], v)
                inst = emit(engobj)
                inst.then_inc(sems[semname], inc)

        @block.tensor
        def _(e):
            run(e, "pe")

        @block.scalar
        def _(e):
            run(e, "act")

        @block.vector
        def _(e):
            run(e, "dve")

        @block.gpsimd
        def _(e):
            run(e, "pool")

        @block.sync
        def _(e):
            run(e, "sp")
            for k, v in final_waits:
                e.wait_ge(sems[k], v)
    return nc


WEIGHT_NAMES = ["norm_mix_g", "w_in", "conv_w", "conv_b", "rg_wa", "rg_ba", "rg_wx", "rg_bx", "rg_lambda",
                "sgu_ln_g", "sgu_ln_b", "sgu_ws", "sgu_bs", "w_proj_a", "w_proj_b", "w_out", "norm_ffn_g",
                "w_gate_up", "w_down", "norm_final_g"]


def run_sharded(inputs, n_main, n_pre, trace=False):
    x = np.ascontiguousarray(np.asarray(inputs["x"], dtype=np.float32))
    B, S, _ = x.shape
    half = S // 2
    assert half == n_main * T and (n_pre == 0 or half == n_pre * T)
    weights = {k: np.ascontiguousarray(np.asarray(inputs[k], dtype=np.float32)) for k in WEIGHT_NAMES}
    nc = build_nc(n_main, n_pre)
    in_maps = []
    for b in range(B):
        for hh in range(2):
            m = dict(weights)
            m["xm"] = np.ascontiguousarray(x[b, hh * half:(hh + 1) * half])
            if hh == 0:
                m["xp"] = np.zeros((max(n_pre, 1) * T, D), np.float32)
                m["flag"] = np.zeros((P, 1), np.float32)
            else:
                m["xp"] = np.ascontiguousarray(x[b, 0:half])
                m["flag"] = np.ones((P, 1), np.float32)
            in_maps.append(m)
    res = run_bass_kernel_spmd(nc, in_maps, core_ids=list(range(2 * B)), trace=trace)
    out = np.empty((B, S, D), np.float32)
    for b in range(B):
        for hh in range(2):
            out[b, hh * half:(hh + 1) * half] = res.results[2 * b + hh]["out"]
    return out, res


def kernel(**inputs):
    out, _ = run_sharded(inputs, 8, 8)
    return out
```
